# Optimizing a Trainium2 kernel written in Bass

```python
import jax, jax.numpy as jnp
from jax import lax
import numpy as np

D_MODEL = 2048
BATCH = 16
SEQ = 2048
DEPTH = 1
DEC_BATCH = 8
DEC_SEQ = 64
PAST_LEN = 2048

CHUNK = 64
N_HEADS_M = 4
DQK = D_MODEL // 16
DV = D_MODEL // 8
D_M = N_HEADS_M * DV
SGU_CHUNK = 128
N_GROUPS_S = 4
D_S = D_MODEL // 2
GS_W = D_S // N_GROUPS_S
D_MIX = D_M + D_S
D_FF = 4 * D_MODEL
QK_W = N_HEADS_M * DQK
D_IN = 2 * QK_W + 2 * D_M + 2 * N_HEADS_M + 2 * D_S
SPLIT_IDX = (QK_W, 2 * QK_W, 2 * QK_W + D_M, 2 * QK_W + 2 * D_M,
             2 * QK_W + 2 * D_M + 2 * N_HEADS_M,
             2 * QK_W + 2 * D_M + 2 * N_HEADS_M + D_S)
EPS = 1e-6

kernel_name = "mlstm_sgu_hybrid_stream_step"


def rmsnorm(x, g):
    xf = x.astype(jnp.float32)
    y = xf * lax.rsqrt(jnp.mean(xf * xf, axis=-1, keepdims=True) + EPS)
    return (y * g.astype(jnp.float32)).astype(x.dtype)


def mlstm_chunkwise(q, k, v, ig, lf, C0, n0, m0, blk_len):
    B, S, H, _ = q.shape
    nc = S // blk_len

    def blk(a):
        a = a.reshape((B, nc, blk_len, H) + a.shape[3:])
        return jnp.moveaxis(a, (1, 3), (0, 2))

    tri = jnp.tril(jnp.ones((blk_len, blk_len), dtype=bool))

    def step(carry, xs):
        C, n, m = carry
        qb, kb, vb, ib, fb = xs
        b = jnp.cumsum(fb, axis=-1)
        a = b + m[..., None]
        d = b[..., :, None] - b[..., None, :] + ib[..., None, :]
        d = jnp.where(tri, d, -jnp.inf)
        m_t = jnp.maximum(a, jnp.max(d, axis=-1))
        w_inter = jnp.exp(a - m_t)
        s = jnp.einsum('bhtd,bhsd->bhts', qb, kb) * jnp.exp(d - m_t[..., None])
        num = (w_inter[..., None] * jnp.einsum('bhtd,bhde->bhte', qb, C)
               + jnp.einsum('bhts,bhse->bhte', s, vb))
        den = w_inter * jnp.einsum('bhtd,bhd->bht', qb, n) + jnp.sum(s, axis=-1)
        h = num / jnp.maximum(jnp.abs(den), jnp.exp(-m_t))[..., None]
        m_new = m_t[..., -1]
        ws = jnp.exp(b[..., -1:] - b + ib - m_new[..., None])
        decay = jnp.exp(b[..., -1] + m - m_new)
        C_new = decay[..., None, None] * C + jnp.einsum('bhs,bhsd,bhse->bhde', ws, kb, vb)
        n_new = decay[..., None] * n + jnp.einsum('bhs,bhsd->bhd', ws, kb)
        return (C_new, n_new, m_new), h

    (C, n, m), hs = lax.scan(step, (C0, n0, m0),
                             (blk(q), blk(k), blk(v), blk(ig), blk(lf)))
    hs = jnp.moveaxis(hs, (0, 2), (1, 3)).reshape(B, S, H, DV)
    return hs, C, n, m


def token_mix(h, C0, n0, m0, w_in, b_gate, g_mh, g_sgu, w_sp, b_sp, w_out, blk_len):
    B, S, _ = h.shape
    f32 = jnp.float32
    p = h @ w_in
    q, k, v, o, gates, u, z = jnp.split(p, SPLIT_IDX, axis=-1)
    q = q.astype(f32).reshape(B, S, N_HEADS_M, DQK)
    k = k.astype(f32).reshape(B, S, N_HEADS_M, DQK) * (DQK ** -0.5)
    v = v.astype(f32).reshape(B, S, N_HEADS_M, DV)
    gates = gates.astype(f32) + b_gate.astype(f32)
    ig = gates[..., :N_HEADS_M]
    lf = jax.nn.log_sigmoid(gates[..., N_HEADS_M:])
    hm, C, n, m = mlstm_chunkwise(q, k, v, ig, lf, C0.astype(f32), n0.astype(f32),
                                  m0.astype(f32), blk_len)
    hm = hm * lax.rsqrt(jnp.mean(hm * hm, axis=-1, keepdims=True) + EPS)
    hm = hm * g_mh.astype(f32).reshape(N_HEADS_M, DV)
    y_m = (hm.reshape(B, S, D_M) * jax.nn.sigmoid(o.astype(f32))).astype(h.dtype)
    u = jax.nn.gelu(u)
    zs = rmsnorm(jax.nn.gelu(z), g_sgu)
    lc = min(S, SGU_CHUNK)
    nch = S // lc
    mask = jnp.tril(jnp.ones((lc, lc), dtype=bool))
    w_s = jnp.where(mask, w_sp[:, :lc, :lc], 0.0)
    zr = zs.reshape(B, nch, lc, N_GROUPS_S, GS_W)
    mix = (jnp.einsum('gts,bnsge->bntge', w_s, zr)
           + b_sp[:, :lc].T[None, None, :, :, None])
    y_s = (u * mix.reshape(B, S, D_S)).astype(h.dtype)
    out = jnp.concatenate([y_m, y_s], axis=-1) @ w_out
    return out, C, n, m, zs


def layer(x, C0, n0, m0, w_in, b_gate, g_mh, g_sgu, w_sp, b_sp, w_out,
          g_norm1, g_norm2, w_ff1, w_ff2, blk_len):
    mix, C, n, m, zs = token_mix(rmsnorm(x, g_norm1), C0, n0, m0, w_in, b_gate, g_mh,
                                 g_sgu, w_sp, b_sp, w_out, blk_len)
    x = x + mix
    hid = jnp.square(jax.nn.relu(rmsnorm(x, g_norm2) @ w_ff1))
    x = x + hid @ w_ff2
    return x, C, n, m, zs


def setup_inputs(seed: int = 0) -> dict:
    key = jax.random.key(seed)
    ks = jax.random.split(key, 20)
    nrm = jax.random.normal
    f_bias = jnp.concatenate([jnp.zeros((N_HEADS_M,), jnp.float32),
                              jnp.linspace(3.0, 6.0, N_HEADS_M, dtype=jnp.float32)])
    return {
        "x_prompt": nrm(ks[0], (BATCH, SEQ, D_MODEL), jnp.float32),
        "x_sample": nrm(ks[1], (DEC_BATCH, DEC_SEQ, D_MODEL), jnp.float32),
        "state_mlstm_C": 0.1 * nrm(ks[2], (DEPTH, DEC_BATCH, N_HEADS_M, DQK, DV), jnp.float32),
        "state_mlstm_n": 0.1 * nrm(ks[3], (DEPTH, DEC_BATCH, N_HEADS_M, DQK), jnp.float32),
        "state_mlstm_m": 0.5 * nrm(ks[4], (DEPTH, DEC_BATCH, N_HEADS_M), jnp.float32),
        "w_in": nrm(ks[5], (DEPTH, D_MODEL, D_IN), jnp.float32) * D_MODEL ** -0.5,
        "b_gate": f_bias + 0.1 * nrm(ks[6], (DEPTH, 2 * N_HEADS_M), jnp.float32),
        "g_mh": 1.0 + 0.05 * nrm(ks[7], (DEPTH, D_M), jnp.float32),
        "g_sgu": 1.0 + 0.05 * nrm(ks[8], (DEPTH, D_S), jnp.float32),
        "w_sp": nrm(ks[9], (DEPTH, N_GROUPS_S, SGU_CHUNK, SGU_CHUNK), jnp.float32) * SGU_CHUNK ** -0.5,
        "b_sp": 1.0 + 0.1 * nrm(ks[10], (DEPTH, N_GROUPS_S, SGU_CHUNK), jnp.float32),
        "w_out": nrm(ks[11], (DEPTH, D_MIX, D_MODEL), jnp.float32) * D_MIX ** -0.5,
        "g_norm1": 1.0 + 0.05 * nrm(ks[12], (DEPTH, D_MODEL), jnp.float32),
        "g_norm2": 1.0 + 0.05 * nrm(ks[13], (DEPTH, D_MODEL), jnp.float32),
        "w_ff1": nrm(ks[14], (DEPTH, D_MODEL, D_FF), jnp.float32) * D_MODEL ** -0.5,
        "w_ff2": nrm(ks[15], (DEPTH, D_FF, D_MODEL), jnp.float32) * D_FF ** -0.5,
        "g_final": 1.0 + 0.05 * nrm(ks[16], (D_MODEL,), jnp.float32),
    }


def reference(x_prompt, x_sample, state_mlstm_C, state_mlstm_n, state_mlstm_m,
              w_in, b_gate, g_mh, g_sgu, w_sp, b_sp, w_out, g_norm1, g_norm2,
              w_ff1, w_ff2, g_final):
    bp = x_prompt.shape[0]
    blk_sample = x_sample.shape[1]
    yp, ys = x_prompt, x_sample
    cp_l, np_l, mp_l, cs_l, ns_l, ms_l, vs_l = [], [], [], [], [], [], []
    for l in range(DEPTH):
        params = (w_in[l], b_gate[l], g_mh[l], g_sgu[l], w_sp[l], b_sp[l], w_out[l],
                  g_norm1[l], g_norm2[l], w_ff1[l], w_ff2[l])
        C0 = jnp.zeros((bp, N_HEADS_M, DQK, DV), jnp.float32)
        n0 = jnp.zeros((bp, N_HEADS_M, DQK), jnp.float32)
        m0 = jnp.zeros((bp, N_HEADS_M), jnp.float32)
        yp, c_p, n_p, m_p, _ = layer(yp, C0, n0, m0, *params, blk_len=CHUNK)
        ys, c_s, n_s, m_s, v_s = layer(ys, state_mlstm_C[l], state_mlstm_n[l],
                                       state_mlstm_m[l], *params, blk_len=blk_sample)
        cp_l.append(c_p); np_l.append(n_p); mp_l.append(m_p)
        cs_l.append(c_s); ns_l.append(n_s); ms_l.append(m_s); vs_l.append(v_s)
    y_prompt = rmsnorm(yp, g_final)
    y_sample = rmsnorm(ys, g_final)
    C_prompt = jnp.stack(cp_l)
    n_prompt = jnp.stack(np_l)
    m_prompt = jnp.stack(mp_l)
    C_sample = jnp.stack(cs_l)
    n_sample = jnp.stack(ns_l)
    m_sample = jnp.stack(ms_l)
    sgu_v_sample = jnp.stack(vs_l)
    return (y_prompt, y_sample, C_prompt, n_prompt, m_prompt, C_sample, n_sample, m_sample, sgu_v_sample)
```

```python
import numpy as np
from contextlib import ExitStack
import concourse.bass as bass
import concourse.mybir as mybir
from concourse.bass_utils import run_bass_kernel_spmd

F32 = mybir.dt.float32
BF16 = mybir.dt.bfloat16
AF = mybir.ActivationFunctionType
ALU = mybir.AluOpType

D = 2048
KC = 16
NH = 4
DQK = 128
DV = 256
DM = 1024
DS = 1024
DFF = 8192
SEQ = 2048
DEC = 64
EPS = 1e-6
NEG = -30000.0
SEG = 20000
NDMASEM = 24
DSZ = {F32: 4, BF16: 2}


def _dsize(dt):
    return 2 if dt == BF16 else 4


def region(ap):
    pat = ap.ap
    ds = _dsize(ap.dtype)
    off = ap.offset
    sp = str(ap.space)
    if "DRAM" in sp:
        span = 1
        for (s, c) in pat:
            span += (c - 1) * s
        return (ap.name, 0, 1, off * ds, (off + span) * ds)
    if "PSUM" in sp:
        return (ap.name, 0, 128, 0, 2048)
    ps, pc = pat[0]
    if ps == 0:
        ps = 1 << 40
    p0 = off // ps
    lo = off % ps
    fr = [(s_, c_) for (s_, c_) in pat[1:] if c_ > 1]
    if len(fr) == 2 and fr[1][0] == 1 and 1 < fr[0][1] <= 16 and fr[0][0] > fr[1][1]:
        s1, c1 = fr[0]
        c2 = fr[1][1]
        return [(ap.name, p0, p0 + pc, (lo + i * s1) * ds, (lo + i * s1 + c2) * ds) for i in range(c1)]
    span = 1
    for (s, c) in pat[1:]:
        span += (c - 1) * s
    return (ap.name, p0, p0 + pc, lo * ds, (lo + span) * ds)


def regions(aps):
    out = []
    for a in aps:
        r = region(a)
        if isinstance(r, list):
            out.extend(r)
        else:
            out.append(r)
    return out


class Sched:
    def __init__(self, nc):
        self.nc = nc
        self.ins = []
        self.acc = {}
        self.frozen = set()
        self.eng_obj = {"pe": nc.tensor, "act": nc.scalar, "dve": nc.vector,
                        "pool": nc.gpsimd, "sp": nc.sync}

    def freeze(self, name):
        self.frozen.add(name)

    def add(self, eng, fn, reads=(), writes=(), dma=False, extra_deps=()):
        idx = len(self.ins)
        deps = set(extra_deps)
        rr = regions(reads)
        ww = regions(writes)
        for (nm, p0, p1, lo, hi) in rr:
            d = self.acc.get(nm)
            if d:
                psum = nm.startswith("ps")
                for (k, j) in d.items():
                    if (k[5] or (psum and k[0] != eng)) and k[1] < p1 and p0 < k[2] and k[3] < hi and lo < k[4]:
                        deps.add(j)
        for (nm, p0, p1, lo, hi) in ww:
            d = self.acc.get(nm)
            if d:
                for (k, j) in d.items():
                    if k[1] < p1 and p0 < k[2] and k[3] < hi and lo < k[4]:
                        deps.add(j)
        tag = idx if dma else eng
        for (nm, p0, p1, lo, hi) in rr:
            if nm in self.frozen:
                continue
            self.acc.setdefault(nm, {})[(tag, p0, p1, lo, hi, False)] = idx
        for (nm, p0, p1, lo, hi) in ww:
            d = self.acc.setdefault(nm, {})
            dead = [k for k in d if k[1] >= p0 and k[2] <= p1 and k[3] >= lo and k[4] <= hi]
            for k in dead:
                del d[k]
            d[(tag, p0, p1, lo, hi, True)] = idx
        deps.discard(idx)
        self.ins.append(dict(eng=eng, fn=fn, deps=deps, dma=dma))
        return idx

    def emit(self, stack):
        nc = self.nc
        ins = self.ins
        n = len(ins)

        def skip(dd, it):
            return (not dd["dma"]) and (not it["dma"]) and dd["eng"] == "pe" and it["eng"] == "pe"

        needed = [False] * n
        for it in ins:
            for d in it["deps"]:
                if not skip(ins[d], it):
                    needed[d] = True
        sems = {}

        def get_sem(key):
            if key not in sems:
                sems[key] = stack.enter_context(nc.semaphore("s%d" % len(sems)))
            return sems[key]

        sig = [None] * n
        eng_cnt = {}
        ndma = 0
        dma_prev = {}
        for k, it in enumerate(ins):
            if it["fn"] is None:
                continue
            if it["dma"]:
                slot = ndma % NDMASEM
                gen = ndma // NDMASEM
                ndma += 1
                sig[k] = (("dma", slot), 16 * (gen + 1))
                it["dma_prev"] = (("dma", slot), 16 * gen) if gen > 0 else None
            elif needed[k]:
                c = eng_cnt.get(it["eng"], 0)
                eng_cnt[it["eng"]] = c + 1
                sig[k] = (("eng", it["eng"], c // SEG), (c % SEG) + 1)
        waited = {e: {} for e in self.eng_obj}
        nwaits = 0
        for k, it in enumerate(ins):
            e = it["eng"]
            eo = self.eng_obj[e]
            need = {}
            for d in it["deps"]:
                if skip(ins[d], it):
                    continue
                skey, val = sig[d]
                if need.get(skey, 0) < val:
                    need[skey] = val
            if it.get("dma_prev"):
                skey, val = it["dma_prev"]
                if need.get(skey, 0) < val:
                    need[skey] = val
            for skey, val in need.items():
                if waited[e].get(skey, 0) >= val:
                    continue
                eo.wait_ge(get_sem(skey), val)
                waited[e][skey] = val
                nwaits += 1
            if it["fn"] is None:
                continue
            r = it["fn"](eo)
            if sig[k] is not None:
                skey, val = sig[k]
                r.then_inc(get_sem(skey), 16 if it["dma"] else 1)
        return dict(n=n, nwaits=nwaits, nsems=len(sems), ndma=ndma, eng_cnt=eng_cnt)


def build_program(n_prompt_tiles=4, do_sample=True, n_seq=2, max_ins=None, pipeline=True):
    nc = bass.Bass("TRN2", target_bir_lowering=False)
    dram = lambda n, s, dt, k: nc.dram_tensor(n, s, dt, kind=k).ap()
    xp = dram("xp", [2, SEQ, D], F32, "ExternalInput")
    xs = dram("xs", [DEC, D], F32, "ExternalInput")
    c0 = dram("c0", [NH, DQK, DV], F32, "ExternalInput")
    n0t = dram("n0t", [DQK, NH], F32, "ExternalInput")
    m0 = dram("m0", [NH, 1], F32, "ExternalInput")
    w_in = dram("w_in", [D, 5128], F32, "ExternalInput")
    w_out = dram("w_out", [D, D], F32, "ExternalInput")
    w_ff1 = dram("w_ff1", [D, DFF], F32, "ExternalInput")
    w_ff2 = dram("w_ff2", [DFF, D], F32, "ExternalInput")
    bgi_d = dram("bgi", [NH, 1], F32, "ExternalInput")
    bgf_d = dram("bgf", [NH, 1], F32, "ExternalInput")
    gmh_d = dram("g_mh", [1, DM], F32, "ExternalInput")
    gsgu_d = dram("g_sgu", [1, DS], F32, "ExternalInput")
    wsp_d = dram("w_sp", [4, 128, 128], F32, "ExternalInput")
    bsp_d = dram("b_sp", [1, 512], F32, "ExternalInput")
    g1t_d = dram("g1t", [128, KC], F32, "ExternalInput")
    g2t_d = dram("g2t", [128, KC], F32, "ExternalInput")
    gf_d = dram("g_final", [1, D], F32, "ExternalInput")

    yp = dram("yp", [2, SEQ, D], F32, "ExternalOutput")
    ys = dram("ys", [DEC, D], F32, "ExternalOutput")
    cp = dram("cp", [2, NH, DQK, DV], F32, "ExternalOutput")
    npt = dram("npt", [2, DQK, NH], F32, "ExternalOutput")
    mp = dram("mp", [2, NH, 1], F32, "ExternalOutput")
    cs_o = dram("cs", [NH, DQK, DV], F32, "ExternalOutput")
    nst_o = dram("nst", [DQK, NH], F32, "ExternalOutput")
    ms_o = dram("ms", [NH, 1], F32, "ExternalOutput")
    vs_o = dram("vs", [DEC, DS], F32, "ExternalOutput")

    NUNIT = 46
    wscr = dram("wscr", [NUNIT, 128, 8192], BF16, "Internal")

    st = ExitStack()
    E = st.enter_context
    sb = lambda n, s, dt: E(nc.sbuf_tensor(n, s, dt))
    xt = sb("xt", [128, 4, D], F32)
    hT = sb("hT", [128, KC, 512], BF16)
    BIGN = 42000
    big = sb("big", [128, BIGN], BF16)
    wr = [sb("wr%d" % i, [128, KC, 512], BF16) for i in range(3)]
    wg = sb("wg", [128, KC, 8], BF16)
    identf = sb("identf", [128, 128], F32)
    identb = sb("identb", [128, 128], BF16)
    sel = sb("sel", [4, 4, 128], F32)
    maskneg = sb("maskneg", [128, 128], BF16)
    ones_bf = sb("ones_bf", [128, 1], BF16)
    onesf = sb("onesf", [1, 128], F32)
    epsc = sb("epsc", [128, 1], F32)
    onec = sb("onec", [128, 1], F32)
    wspT = sb("wspT", [128, 4, 128], BF16)
    bsp_hi = sb("bsp_hi", [1, 512], BF16)
    bsp_lo = sb("bsp_lo", [1, 512], BF16)
    ones_b1 = sb("ones_b1", [1, 128], BF16)
    gsgu_bc = sb("gsgu_bc", [128, DS], F32)
    gmh_bc = sb("gmh_bc", [128, DM], F32)
    g1t = sb("g1t_sb", [128, KC], F32)
    g2t = sb("g2t_sb", [128, KC], F32)
    bgi = sb("bgi_sb", [4, 1], F32)
    bgf = sb("bgf_sb", [4, 1], F32)
    Cst = sb("Cst", [128, NH, DV + 1], F32)
    Cbf = sb("Cbf", [128, NH, DV + 1], BF16)
    denS = sb("denS", [128, 2, NH], F32)
    Bcar = sb("Bcar", [4, 1], F32)
    Mcar = sb("Mcar", [4, 1], F32)
    mlast = sb("mlast", [4, 1], F32)
    Mb = sb("Mb", [4, 8], F32)
    Me = sb("Me", [4, 8], F32)
    cols = sb("cols", [128, 4, 8], F32)
    dec = sb("dec", [128, NH, 8], F32)
    ss1 = sb("ss1", [128, 4], F32)
    rs1 = sb("rs1", [128, 4], F32)
    ssz = sb("ssz", [128, 4, 2], F32)
    rz = sb("rz", [128, 4], F32)
    ssn = sb("ssn", [128, 4], F32)
    ssP2 = sb("ssP2", [128, 2], F32)
    ssP = sb("ssP", [128, 4], F32)
    rsP = sb("rsP", [128, 4], F32)
    sm_a = sb("sm_a", [128, 4], F32)
    sm_b = sb("sm_b", [128, 4], F32)
    sm_f = sb("sm_f", [128, 4], F32)
    tmpA = [sb("tmpA%d" % i, [128, 512], F32) for i in range(2)]
    junk2 = sb("junk2", [128, 512], BF16)
    ps = [E(nc.psum_tensor("ps%d" % i, [128, 512], F32)) for i in range(8)]

    def bview(lo, n, dt=BF16):
        v = big[:, lo:lo + n]
        return v.bitcast(F32) if dt == F32 else v

    junk = bview(0, 2048)
    hn = [bview(2048, 2048), bview(4096, 2048)]
    qT = bview(0, 2048).rearrange("p (h t) -> p h t", h=4)
    qh = bview(2048, 2048).rearrange("p (h t) -> p h t", h=4)
    kT = bview(4096, 2048).rearrange("p (h t) -> p h t", h=4)
    kw = bview(6144, 2048).rearrange("p (s h d) -> p s h d", s=4, h=4)
    vv = bview(8192, 4112).rearrange("p (s h d) -> p s h d", s=4, h=4)
    gso = bview(12304, 4096).rearrange("p (s c) -> p s c", s=4)
    uTg = bview(16400, 4096).rearrange("p (c t) -> p c t", c=8)
    zs = bview(20496, 4096).rearrange("p (s c) -> p s c", s=4)
    ym = bview(24592, 4096).rearrange("p (s c) -> p s c", s=4)
    sT = bview(28688, 2048).rearrange("p (h j t) -> p h j t", h=4, j=4)
    DT = None
    Wbc = [bview(30736 + 1024 * i, 1024, F32) for i in range(2)]
    gzS = bview(32784, 2048, F32)
    wspS = [bview(34832 + 512 * i, 512).rearrange("p (g t) -> p g t", g=4) for i in range(2)]
    rows = [bview(35856 + 1024 * i, 1024, F32) for i in range(6)]
    gf_bc = bview(37904, 4096, F32)
    xstg = bview(32768, 4096, F32)
    hid = bview(0, 32768).rearrange("p (c t) -> p c t", c=64)
    wstage = [bview(16384 * i, 16384, F32).rearrange("p (k c) -> p k c", k=KC) for i in range(2)]

    LM = dict(junk=junk, hn=hn, qT=qT, qh=qh, kT=kT, kw=kw, vv=vv, gso=gso, uTg=uTg, zs=zs, ym=ym, sT=sT, DT=DT,
              Wbc=Wbc, gzS=gzS, wspS=wspS, rows=rows, gf_bc=gf_bc, hid=hid)
    _o = [32768]

    def sv(n, dt=BF16):
        v = bview(_o[0], n, dt)
        _o[0] += n
        return v
    _hn0 = xt[:, 2, 1024:2048].bitcast(BF16)
    LS = dict(
        qT=sv(256).rearrange("p (h t) -> p h t", h=4), qh=sv(256).rearrange("p (h t) -> p h t", h=4),
        kT=sv(256).rearrange("p (h t) -> p h t", h=4),
        kw=sv(512).rearrange("p (s h d) -> p s h d", s=1, h=4),
        vv=sv(1028).rearrange("p (s h d) -> p s h d", s=1, h=4),
        gso=sv(1024).rearrange("p (s c) -> p s c", s=1),
        uTg=sv(512).rearrange("p (c t) -> p c t", c=8),
        zs=sv(1024).rearrange("p (s c) -> p s c", s=1),
        ym=sv(1024).rearrange("p (s c) -> p s c", s=1),
        sT=sv(256).rearrange("p (h j t) -> p h j t", h=4, j=1),
        DT=[sv(128, F32).rearrange("p (j t) -> p j t", j=1) for _ in range(2)],
        Wbc=[sv(128, F32) for _ in range(2)],
        wspS=[sv(512).rearrange("p (g t) -> p g t", g=4) for _ in range(2)],
        rows=[sv(128, F32) for _ in range(6)],
        hid=bview(32768, 4096).rearrange("p (c t) -> p c t", c=64),
        gf_bc=xt[:, 1, :], junk=xt[:, 2, 0:1024].bitcast(BF16), hn=[_hn0, _hn0], gzS=xt[:, 3, 0:1024],
    )
    assert _o[0] <= BIGN
    S = Sched(nc)
    A = S.add
    marks = []
    mark = lambda l: marks.append((l, len(S.ins)))

    def dma(out, in_, eng="sp", slow=False):
        if slow:
            return A(eng, lambda e: e.dma_start(out=out, in_=in_, allow_slow_non_contiguous=True),
                     reads=[in_], writes=[out], dma=True)
        return A(eng, lambda e: e.dma_start(out=out, in_=in_), reads=[in_], writes=[out], dma=True)

    def mm(out, lhsT, rhs, start=True, stop=True):
        return A("pe", lambda e: e.matmul(out, lhsT=lhsT, rhs=rhs, start=start, stop=stop),
                 reads=[lhsT, rhs], writes=[out])

    def act(out, in_, func, scale=1.0, bias=None, accum=None, extra_r=()):
        rd = [in_] + list(extra_r)
        wr_ = [out]
        kw_ = dict(out=out, in_=in_, func=func)
        if not isinstance(scale, float):
            rd.append(scale)
        kw_["scale"] = scale
        if bias is not None:
            kw_["bias"] = bias
            rd.append(bias)
        if accum is not None:
            kw_["accum_out"] = accum
            wr_.append(accum)
        return A("act", lambda e: e.activation(**kw_), reads=rd, writes=wr_)

    def tt(eng, out, in0, in1, op):
        return A(eng, lambda e: e.tensor_tensor(out=out, in0=in0, in1=in1, op=op), reads=[in0, in1], writes=[out])

    def ts_(eng, out, in0, s1, s2, op0, op1=None):
        rd = [in0] + [s for s in (s1, s2) if s is not None and not isinstance(s, float)]
        if op1 is None:
            s2, op1 = 0.0, ALU.add
        return A(eng, lambda e: e.tensor_scalar(out=out, in0=in0, scalar1=s1, scalar2=s2, op0=op0, op1=op1),
                 reads=rd, writes=[out])

    def stt(eng, out, in0, scalar, in1, op0, op1):
        rd = [in0, in1] + ([] if isinstance(scalar, float) else [scalar])
        return A(eng, lambda e: e.scalar_tensor_tensor(out=out, in0=in0, scalar=scalar, in1=in1, op0=op0, op1=op1),
                 reads=rd, writes=[out])

    def cp_(eng, out, in_):
        if eng == "act":
            return A("act", lambda e: e.copy(out=out, in_=in_), reads=[in_], writes=[out])
        return A(eng, lambda e: e.tensor_copy(out=out, in_=in_), reads=[in_], writes=[out])

    def memset(eng, ap, val):
        return A(eng, lambda e: e.memset(ap, val), writes=[ap])

    def asel(out, in_, pattern, op, fill, base, cm):
        return A("pool", lambda e: e.affine_select(out=out, in_=in_, pattern=pattern, compare_op=op, fill=fill,
                                                   base=base, channel_multiplier=cm), reads=[in_], writes=[out])

    out_dmas = []

    memset("pool", identf[:], 0.0)
    asel(identf[:], identf[:], [[-1, 128]], ALU.not_equal, 1.0, 0, 1)
    cp_("dve", identb[:], identf[:])
    memset("pool", sel[:], 0.0)
    asel(sel[:], sel[:], [[-1, 4], [0, 128]], ALU.not_equal, 1.0, 0, 1)
    maskf = bview(4096, 256, F32)
    memset("pool", maskf, 0.0)
    asel(maskf, maskf, [[1, 128]], ALU.is_ge, NEG, 0, -1)
    cp_("dve", maskneg[:], maskf)
    memset("pool", ones_bf[:], 1.0)
    memset("pool", onesf[:], 1.0)
    memset("pool", epsc[:], EPS)
    memset("pool", onec[:], 1.0)
    dma(bgi[:], bgi_d[:, :])
    dma(bgf[:], bgf_d[:, :])
    dma(g1t[:], g1t_d[:, :])
    dma(g2t[:], g2t_d[:, :])
    bsp = bview(8192, 1024, F32)[0:1, :]
    bsp_t = bview(9216, 1024, F32)[0:1, :]
    dma(bsp, bsp_d[:, :])
    cp_("dve", bsp_hi[:], bsp)
    cp_("dve", bsp_t, bsp_hi[:])
    tt("dve", bsp_t, bsp, bsp_t, ALU.subtract)
    cp_("dve", bsp_lo[:], bsp_t)
    memset("pool", ones_b1[:], 1.0)
    dma(gsgu_bc[:], gsgu_d[0:1, :].to_broadcast([128, DS]))
    dma(gmh_bc[:], gmh_d[0:1, :].to_broadcast([128, DM]))
    ts_("dve", gmh_bc[:], gmh_bc[:], 0.5, None, ALU.mult)
    wspf = bview(0, 1024, F32).rearrange("p (g s) -> p g s", g=4)
    wspb = bview(1024, 512).rearrange("p (g s) -> p g s", g=4)
    dma(wspf, wsp_d.rearrange("g t s -> t g s"))
    asel(wspf, wspf, [[0, 4], [-1, 128]], ALU.is_ge, 0.0, 0, 1)
    cp_("dve", wspb, wspf)
    pst = ps[0][:, 0:256].bitcast(BF16).rearrange("p (g t) -> p g t", g=4)
    for g in range(4):
        A("pe", lambda e, g=g: e.transpose(out=pst[:, g, :], in_=wspb[:, g, :], identity=identb[:]),
          reads=[wspb[:, g, :], identb[:]], writes=[pst[:, g, :]])
    cp_("dve", wspT[:], pst)

    mark('prepass')
    units = []
    win_v = w_in.rearrange("(k p) c -> p k c", p=128)
    col0 = {"q": 0, "k": 512, "v0": 1024, "v1": 1536, "o0": 2048, "o1": 2560,
            "u0": 3080, "u1": 3592, "z0": 4104, "z1": 4616}
    UID = {}
    for nm in ["v0", "v1", "o0", "o1", "z0", "z1", "u0", "u1", "k", "q"]:
        UID[nm] = len(units)
        units.append((win_v[:, :, col0[nm]:col0[nm] + 512], g1t))
    wout_v = w_out.rearrange("(k p) c -> p k c", p=128)
    for c in range(4):
        UID["wo%d" % c] = len(units)
        units.append((wout_v[:, :, c * 512:(c + 1) * 512], None))
    wff1_v = w_ff1.rearrange("(k p) c -> p k c", p=128)
    for c in range(16):
        UID["f1_%d" % c] = len(units)
        units.append((wff1_v[:, :, c * 512:(c + 1) * 512], g2t))
    wff2_v = w_ff2.rearrange("(q k p) c -> q p k c", q=4, p=128)
    for c in range(4):
        for q in range(4):
            UID["f2_%d_%d" % (c, q)] = len(units)
            units.append((wff2_v[q][:, :, c * 512:(c + 1) * 512], None))
    assert len(units) == NUNIT
    gst = bview(32768, 256, F32).rearrange("p (k c) -> p k c", k=KC)
    dma(gst, win_v[:, :, 3072:3080])
    tt("dve", wg[:], gst, g1t[:].unsqueeze(2).to_broadcast([128, KC, 8]), ALU.mult)
    def stage_load(u):
        src = units[u][0]
        stg = wstage[u % 2]
        for q4 in range(4):
            dma(stg[:, q4 * 4:(q4 + 1) * 4, :], src[:, q4 * 4:(q4 + 1) * 4, :])

    pre = dict(cast=0)

    def do_cast(u):
        src, gt = units[u]
        stg = wstage[u % 2]
        if u == 0:
            stage_load(0)
        if u + 1 < len(units):
            stage_load(u + 1)
        dst = wr[u % 3]
        if gt is None:
            half = 8
            cp_("dve", dst[:, 0:half, :], stg[:, 0:half, :])
            cp_("pool", dst[:, half:KC, :], stg[:, half:KC, :])
        else:
            for k in range(KC):
                eng = "dve" if k % 4 != 3 else "pool"
                ts_(eng, dst[:, k, :], stg[:, k, :], gt[:, k:k + 1], None, ALU.mult)
        dma(wscr[u], dst[:].rearrange("p k c -> p (k c)"))

    def pump(upto):
        while pre["cast"] <= min(upto, len(units) - 1):
            do_cast(pre["cast"])
            pre["cast"] += 1

    fused = bool(do_sample)
    if not fused:
        pump(len(units) - 1)
        S.freeze("wscr")
    mark('tiles')

    wstate = dict(n=0, fused=False, pf=[])

    def load_unit(name):
        mark('unit ' + name)
        if wstate["fused"]:
            u = UID[name]
            pump(u + 2)
            wstate["n"] = u + 1
            return wr[u % 3]
        pf = wstate["pf"]
        if pf and pf[0][0] == name:
            return pf.pop(0)[1]
        slot = wr[wstate["n"] % 3]
        wstate["n"] += 1
        dma(slot[:].rearrange("p k c -> p (k c)"), wscr[UID[name]])
        return slot

    def prefetch_unit(name):
        slot = wr[wstate["n"] % 3]
        wstate["n"] += 1
        dma(slot[:].rearrange("p k c -> p (k c)"), wscr[UID[name]])
        wstate["pf"].append((name, slot))

    def run_tile(T, xsrc, ydst, first, last, kind, seq, skip_norm1=False, nxt=None, xstg_ov=None):
        TS = (T + 127) // 128
        PT = min(T, 128)
        LC = min(T, 128)
        NCH = T // LC
        NJ = TS
        sample = kind == "sample"
        L = LS if (sample and wstate["fused"]) else LM
        junk, hn, qT, qh, kT, kw, vv, gso, uTg, zs, ym, sT, DT, Wbc, gzS, wspS, rows, gf_bc, hid = [
            L[k_] for k_ in ("junk", "hn", "qT", "qh", "kT", "kw", "vv", "gso", "uTg", "zs", "ym", "sT", "DT", "Wbc",
                             "gzS", "wspS", "rows", "gf_bc", "hid")]
        def load_x():
            for ts in range(TS):
                dma(xt[0:PT, ts, :], xsrc(ts))
        if not skip_norm1:
            load_x()

        def rmsnorm_to_hT(ssbuf, rsbuf):
            memset("pool", ssbuf[:, :], 0.0)
            for ts in range(TS):
                act(junk[0:PT, :], xt[0:PT, ts, :], AF.Square, accum=ssbuf[0:PT, ts:ts + 1])
                act(rsbuf[0:PT, ts:ts + 1], ssbuf[0:PT, ts:ts + 1], AF.Ln, scale=1.0 / D, bias=epsc[0:PT, :])
                act(rsbuf[0:PT, ts:ts + 1], rsbuf[0:PT, ts:ts + 1], AF.Exp, scale=-0.5)
                h_ = hn[ts % 2]
                ts_("dve", h_[0:PT, :], xt[0:PT, ts, :], rsbuf[0:PT, ts:ts + 1], None, ALU.mult)
                for half in range(2):
                    pb = ps[(ts * 2 + half) % 4][:, 0:512].bitcast(BF16).rearrange("p (k t) -> p k t", k=8)
                    for k8 in range(8):
                        k = half * 8 + k8
                        A("pe", lambda e, pb=pb, k8=k8, k=k, h_=h_: e.transpose(
                            out=pb[:, k8, 0:PT], in_=h_[0:PT, k * 128:(k + 1) * 128], identity=identb[0:PT, 0:PT]),
                          reads=[h_[0:PT, k * 128:(k + 1) * 128], identb[0:PT, 0:PT]], writes=[pb[:, k8, 0:PT]])
                    cp_("act" if half == 0 else "dve", hT[:, half * 8:(half + 1) * 8, ts * 128:ts * 128 + PT],
                        pb[:, :, 0:PT])

        mark('phase 1: norm1')
        if not skip_norm1:
            rmsnorm_to_hT(ss1, rs1)

        mark('state init')
        if first:
            if sample:
                dma(Cst[:, :, 0:DV], c0.rearrange("h d e -> d h e"))
                dma(Cst[:, :, DV], n0t[:, :], slow=True)
                dma(Mcar[:], m0[:, :])
            else:
                memset("pool", Cst[:], 0.0)
                memset("pool", Mcar[:], 0.0)
            memset("pool", Bcar[:], 0.0)
            cp_("pool", Cbf[:], Cst[:])

        mark('phase 2: input projection')
        def do_gates():
            for k in range(KC):
                mm(ps[4][0:4, 0:T], wg[:, k, 0:4], hT[:, k, 0:T], start=(k == 0), stop=(k == KC - 1))
            for k in range(KC):
                mm(ps[5][0:4, 0:T], wg[:, k, 4:8], hT[:, k, 0:T], start=(k == 0), stop=(k == KC - 1))
            r0, r1, r2, r3, r4, rzero = [r[0:4, 0:T] for r in rows]
            memset("pool", rzero, 0.0)
            ts_("dve", r0, ps[4][0:4, 0:T], bgi[:, 0:1], None, ALU.add)
            ts_("dve", r1, ps[5][0:4, 0:T], bgf[:, 0:1], None, ALU.add)
            act(r2, r1, AF.Abs)
            act(r2, r2, AF.Exp, scale=-1.0)
            act(r2, r2, AF.Ln, bias=onec[0:4, :])
            ts_("dve", r1, r1, 0.0, None, ALU.min)
            tt("dve", r1, r1, r2, ALU.subtract)
            A("dve", lambda e: e.tensor_tensor_scan(out=r2, data0=r1, data1=rzero, initial=Bcar[:, 0:1],
                                                    op0=ALU.add, op1=ALU.add),
              reads=[r1, rzero, Bcar[:, 0:1]], writes=[r2])
            tt("dve", r0, r0, r2, ALU.subtract)
            A("dve", lambda e: e.tensor_tensor_scan(out=r1, data0=r0, data1=r0, initial=Mcar[:, 0:1],
                                                    op0=ALU.max, op1=ALU.max),
              reads=[r0, Mcar[:, 0:1]], writes=[r1])
            tt("dve", r3, r2, r1, ALU.add)
            cp_("dve", mlast[:, 0:1], r3[:, T - 1:T])
            cp_("dve", Mb[:, 0:1], Mcar[:, 0:1])
            if NCH > 1:
                cp_("dve", Mb[:, 1:NCH], r1.rearrange("p (c t) -> p c t", t=LC)[:, 0:NCH - 1, LC - 1])
            cp_("dve", Me[:, 0:NCH], r1.rearrange("p (c t) -> p c t", t=LC)[:, :, LC - 1])
            cp_("dve", Bcar[:, 0:1], r2[:, T - 1:T])
            cp_("dve", Mcar[:, 0:1], r1[:, T - 1:T])
            act(r3, r3, AF.Exp, scale=-1.0)
            r1c = r1.rearrange("p (c t) -> p c t", t=LC)
            tt("dve", r2.rearrange("p (c t) -> p c t", t=LC), Mb[:, 0:NCH].unsqueeze(2).to_broadcast([4, NCH, LC]),
               r1c, ALU.subtract)
            act(r2, r2, AF.Exp)
            tt("dve", r4.rearrange("p (c t) -> p c t", t=LC), r0.rearrange("p (c t) -> p c t", t=LC),
               Me[:, 0:NCH].unsqueeze(2).to_broadcast([4, NCH, LC]), ALU.subtract)
            act(r4, r4, AF.Exp)
            ts_("dve", r1, r1, -1.0, None, ALU.mult)
            gt_r, negM_r, wi_r, em_r, ws_r = r0, r1, r2, r3, r4
            return r0, r1, r2, r3, r4

        def do_cols(ws_r, em_r):
            pcol = ps[6][:, 0:32].rearrange("p (s c) -> p s c", s=4)
            for ts in range(TS):
                mm(pcol[0:PT, ts, 0:4], ws_r[:, ts * 128:ts * 128 + PT], identf[0:4, 0:4])
                mm(pcol[0:PT, ts, 4:8], em_r[:, ts * 128:ts * 128 + PT], identf[0:4, 0:4])
            cp_("dve", cols[0:PT, 0:TS, :], pcol[0:PT, 0:TS, :])

        def tok_major(unit, evac):
            slot = load_unit(unit)
            for ts in range(TS):
                for k in range(KC):
                    mm(ps[ts][0:PT, :], hT[:, k, ts * 128:ts * 128 + PT], slot[:, k, :],
                       start=(k == 0), stop=(k == KC - 1))
                evac(ts, ps[ts])

        def feat_major(unit, evac):
            slot = load_unit(unit)
            for sub in range(4):
                p_ = ps[sub]
                for k in range(KC):
                    mm(p_[:, 0:T], slot[:, k, sub * 128:(sub + 1) * 128], hT[:, k, 0:T],
                       start=(k == 0), stop=(k == KC - 1))
                evac(sub, p_)

        cnt = dict(a=0)

        def ev_v(sl):
            def f(ts, p_):
                eng = "act" if (ts % 2 == 0) else "dve"
                cp_(eng, vv[0:PT, ts, 2 * sl:2 * sl + 2, 0:DV], p_[0:PT, :].rearrange("p (h d) -> p h d", h=2))
            return f

        def ev_o(sl):
            def f(ts, p_):
                t_ = tmpA[cnt["a"] % 2]
                cnt["a"] += 1
                act(t_[0:PT, :], p_[0:PT, :], AF.Tanh, scale=0.5)
                stt("dve", gso[0:PT, ts, sl * 512:(sl + 1) * 512], t_[0:PT, :], 1.0,
                    gmh_bc[0:PT, sl * 512:(sl + 1) * 512], ALU.add, ALU.mult)
            return f

        def ev_z(sl):
            def f(ts, p_):
                t_ = tmpA[cnt["a"] % 2]
                cnt["a"] += 1
                act(t_[0:PT, :], p_[0:PT, :], AF.Gelu)
                tt("dve", zs[0:PT, ts, sl * 512:(sl + 1) * 512], t_[0:PT, :],
                   gsgu_bc[0:PT, sl * 512:(sl + 1) * 512], ALU.mult)
                act(junk2[0:PT, :], t_[0:PT, :], AF.Square, accum=ssz[0:PT, ts, sl:sl + 1])
                if sample:
                    cp_("pool", gzS[0:PT, sl * 512:(sl + 1) * 512], t_[0:PT, :])
            return f

        def ev_u(sl):
            def f(sub, p_):
                act(uTg[:, sl * 4 + sub, 0:T], p_[:, 0:T], AF.Gelu)
            return f

        def ev_kT(h, p_):
            act(kT[:, h, 0:T], p_[:, 0:T], AF.Copy, scale=float(DQK) ** -0.5)

        def ev_kw(ts, p_):
            for h in range(NH):
                ts_("dve", kw[0:PT, ts, h, :], p_[0:PT, h * 128:(h + 1) * 128], cols[0:PT, ts, h:h + 1],
                    float(DQK) ** -0.5, ALU.mult, ALU.mult)

        memset("pool", ssz[:], 0.0)
        tok_major("v0", ev_v(0))
        tok_major("v1", ev_v(1))
        gt_r, negM_r, wi_r, em_r, ws_r = do_gates()
        tok_major("o0", ev_o(0))
        tok_major("o1", ev_o(1))
        if skip_norm1:
            load_x()
        tok_major("z0", ev_z(0))
        tok_major("z1", ev_z(1))
        do_cols(ws_r, em_r)
        feat_major("u0", ev_u(0))
        feat_major("u1", ev_u(1))
        slot = load_unit("k")
        for ts in range(TS):
            for k in range(KC):
                mm(ps[ts][0:PT, :], hT[:, k, ts * 128:ts * 128 + PT], slot[:, k, :], start=(k == 0), stop=(k == KC - 1))
            ev_kw(ts, ps[ts])
            act(ym[0:PT, ts, 0:512], ps[ts][0:PT, :], AF.Copy, scale=float(DQK) ** -0.5)
            pbk = ps[4 + (ts % 2)][:, 0:256].bitcast(BF16).rearrange("p (h t) -> p h t", h=4)
            for h in range(NH):
                A("pe", lambda e, pbk=pbk, h=h, ts=ts: e.transpose(out=pbk[:, h, 0:PT], in_=ym[0:PT, ts, h * 128:(h + 1) * 128],
                                                                     identity=identb[0:PT, 0:PT]),
                  reads=[ym[0:PT, ts, h * 128:(h + 1) * 128], identb[0:PT, 0:PT]], writes=[pbk[:, h, 0:PT]])
            cp_("act" if ts % 2 == 0 else "dve", kT[:, 0:NH, ts * 128:ts * 128 + PT], pbk[:, :, 0:PT])
        for h in range(NH):
            mm(ps[4 + h][:, 0:T], sel[0:4, h, :], wi_r)
        def pre_phase(h):
            P_ = ps[h][:, :].rearrange("p (j t) -> p j t", j=4)
            Sc = ps[4 + h][:, :].rearrange("p (j t) -> p j t", j=4)
            mm(P_[0:PT, 0:NCH, 0:LC], sel[0:4, h, 0:PT], negM_r.rearrange("p (j t) -> p j t", t=LC), start=True, stop=False)
            for j in range(NCH):
                mm(P_[0:PT, j, 0:LC], gt_r[:, j * LC:(j + 1) * LC], sel[0:4, h, 0:LC], start=False, stop=False)
            mm(P_[0:PT, 0:NCH, 0:LC], identb[0:PT, 0:PT],
               maskneg[0:PT, 0:LC].unsqueeze(1).to_broadcast([PT, NCH, LC]), start=False, stop=True)
            for j in range(NCH):
                mm(Sc[0:PT, j, 0:LC], kT[:, h, j * LC:(j + 1) * LC], qT[:, h, j * LC:(j + 1) * LC])
            dt_ = tmpA[h % 2][:, :].rearrange("p (j t) -> p j t", j=4)
            act(dt_[0:PT, 0:NCH, 0:LC], P_[0:PT, 0:NCH, 0:LC], AF.Exp)
            tt("dve", sT[0:PT, h, 0:NCH, 0:LC], Sc[0:PT, 0:NCH, 0:LC], dt_[0:PT, 0:NCH, 0:LC], ALU.mult)

        slot = load_unit("q")
        for h in range(NH):
            for k in range(KC):
                mm(ps[h][:, 0:T], slot[:, k, h * 128:(h + 1) * 128], hT[:, k, 0:T], start=(k == 0), stop=(k == KC - 1))
            wb = Wbc[h % 2]
            cp_("act", wb[:, 0:T], ps[4 + h][:, 0:T])
            cp_("dve", dec[:, h, 0:NCH], wb[:, 0:T].rearrange("p (c t) -> p c t", t=LC)[:, :, LC - 1])
            cp_("act", qT[:, h, 0:T], ps[h][:, 0:T])
            tt("dve", qh[:, h, 0:T], ps[h][:, 0:T], wb[:, 0:T], ALU.mult)
            if h >= 1:
                pre_phase(h - 1)
        pre_phase(NH - 1)

        mark('phase 3: mixers')
        if wstate["fused"]:
            pump(UID["q"] + 3)
        tt("dve", rz[0:PT, 0:TS], ssz[0:PT, 0:TS, 0], ssz[0:PT, 0:TS, 1], ALU.add)
        act(rz[0:PT, 0:TS], rz[0:PT, 0:TS], AF.Ln, scale=1.0 / DS, bias=epsc[0:PT, :])
        act(rz[0:PT, 0:TS], rz[0:PT, 0:TS], AF.Exp, scale=-0.5)
        if sample:
            stt("dve", gzS[0:PT, :], gzS[0:PT, :], rz[0:PT, 0:1], gsgu_bc[0:PT, :], ALU.mult, ALU.mult)
            out_dmas.append(dma(vs_o[:, :], gzS[0:PT, :]))
        ymT = hT

        def sgu_prep(ts):
            wS = wspS[ts % 2]
            ts_("dve", wS[0:PT, :, 0:PT], wspT[0:PT, :, 0:PT], rz[0:PT, ts:ts + 1], None, ALU.mult)

        def sgu(ts):
            wS = wspS[ts % 2]
            for half in range(2):
                pg = ps[6 + half][:, :].rearrange("p (c t) -> p c t", c=4)
                pg4 = ps[6 + half][:, :].rearrange("p (a b t) -> p a b t", a=2, b=2)
                bh = bsp_hi[0:1, 2 * half * 128:(2 * half + 2) * 128].rearrange("p (g t) -> p g t", g=2)[:, :, 0:PT]
                bl = bsp_lo[0:1, 2 * half * 128:(2 * half + 2) * 128].rearrange("p (g t) -> p g t", g=2)[:, :, 0:PT]
                mm(pg4[:, :, :, 0:PT], ones_b1[0:1, :], bh.unsqueeze(2).to_broadcast([1, 2, 2, PT]), start=True, stop=False)
                mm(pg4[:, :, :, 0:PT], ones_b1[0:1, :], bl.unsqueeze(2).to_broadcast([1, 2, 2, PT]), start=False, stop=False)
                for c4 in range(4):
                    cc = half * 4 + c4
                    g = cc // 2
                    mm(pg[:, c4, 0:PT], zs[0:PT, ts, cc * 128:(cc + 1) * 128], wS[0:PT, g, 0:PT], start=False, stop=(c4 == 3))
                tt("dve", ymT[:, 8 + half * 4:8 + half * 4 + 4, ts * 128:ts * 128 + PT], pg[:, :, 0:PT],
                   uTg[:, half * 4:half * 4 + 4, ts * 128:ts * 128 + PT], ALU.mult)

        mark('chunks')
        memset("pool", vv[:, :, :, DV:DV + 1], 1.0)
        def nsb_of(c):
            nb_ = tmpA if c % 2 == 0 else Wbc
            return [nb_[0][:, 0:512].rearrange("p (h d) -> p h d", h=2), nb_[1][:, 0:512].rearrange("p (h d) -> p h d", h=2)]

        def post(c, defer_pe=False):
            j = c
            nsb = nsb_of(c)
            dS = denS[:, c % 2, :]
            memset("pool", ssn[:], 0.0)
            for h in range(NH):
                act(junk2[0:PT, 0:DV], nsb[h // 2][0:PT, h % 2, :], AF.Square, accum=ssn[0:PT, h:h + 1])
            act(sm_a[0:PT, :], dS[0:PT, :], AF.Abs)
            tt("dve", sm_a[0:PT, :], sm_a[0:PT, :], cols[0:PT, j, 4:8], ALU.max)
            A("dve", lambda e: e.reciprocal(out=sm_a[0:PT, :], in_=sm_a[0:PT, :]),
              reads=[sm_a[0:PT, :]], writes=[sm_a[0:PT, :]])
            tt("dve", sm_b[0:PT, :], sm_a[0:PT, :], sm_a[0:PT, :], ALU.mult)
            tt("dve", sm_b[0:PT, :], sm_b[0:PT, :], ssn[0:PT, :], ALU.mult)
            act(sm_b[0:PT, :], sm_b[0:PT, :], AF.Ln, scale=1.0 / DV, bias=epsc[0:PT, :])
            act(sm_b[0:PT, :], sm_b[0:PT, :], AF.Exp, scale=-0.5)
            tt("dve", sm_f[0:PT, :], sm_a[0:PT, :], sm_b[0:PT, :], ALU.mult)
            for h in range(NH):
                stt("dve", ym[0:PT, j, h * DV:(h + 1) * DV], nsb[h // 2][0:PT, h % 2, :], sm_f[0:PT, h:h + 1],
                    gso[0:PT, j, h * DV:(h + 1) * DV], ALU.mult, ALU.mult)
            if not defer_pe:
                post_pe(c)

        def post_pe(c):
            j = c
            pb7 = ps[7][:, 0:512].bitcast(BF16).rearrange("p (k t) -> p k t", k=8)
            for k8 in range(8):
                A("pe", lambda e, k8=k8, j=j: e.transpose(out=pb7[:, k8, 0:PT], in_=ym[0:PT, j, k8 * 128:(k8 + 1) * 128],
                                                           identity=identb[0:PT, 0:PT]),
                  reads=[ym[0:PT, j, k8 * 128:(k8 + 1) * 128], identb[0:PT, 0:PT]], writes=[pb7[:, k8, 0:PT]])
            cp_("act", ymT[:, 0:8, j * 128:j * 128 + PT], pb7[:, :, 0:PT])

        for c in range(NCH):
            j = c
            sgu_prep(j)
            for h in range(NH):
                nump = ps[h][0:PT, 0:DV + 1]
                qc = qh[:, h, c * LC:(c + 1) * LC]
                sc = sT[0:PT, h, j, 0:LC]
                mm(nump, qc, Cbf[:, h, :], start=True, stop=False)
                mm(nump, sc, vv[0:PT, j, h, :], start=False, stop=True)
                up = ps[4 + (h % 2)]
                mm(up[:, 0:DV + 1], kw[0:PT, j, h, :], vv[0:PT, j, h, :])
                stt("dve", Cst[:, h, :], Cst[:, h, :], dec[:, h, c:c + 1], up[:, 0:DV + 1], ALU.mult, ALU.add)
                cp_("act", Cbf[:, h, :], Cst[:, h, :])
            nsb = nsb_of(c)
            for h in range(NH):
                cp_("act" if h % 2 == 0 else "dve", nsb[h // 2][0:PT, h % 2, :], ps[h][0:PT, 0:DV])
                cp_("act" if h % 2 == 0 else "dve", denS[0:PT, c % 2, h:h + 1], ps[h][0:PT, DV:DV + 1])
            sgu(j)
            if c > 0:
                post(c - 1)
        post(NCH - 1, defer_pe=True)
        mark('phase 4: output projection')
        for cs in range(4):
            slot = load_unit("wo%d" % cs)
            for ts in range(TS):
                if cs == 0 and ts == TS - 1:
                    post_pe(NCH - 1)
                for k in range(KC):
                    mm(ps[ts][0:PT, :], ymT[:, k, ts * 128:ts * 128 + PT], slot[:, k, :], start=(k == 0), stop=(k == KC - 1))
                xs_ = xt[0:PT, ts, cs * 512:(cs + 1) * 512]
                tt("dve", xs_, xs_, ps[ts][0:PT, :], ALU.add)

        if last:
            if sample:
                out_dmas.append(dma(cs_o.rearrange("h d e -> d h e"), Cst[:, :, 0:DV]))
                out_dmas.append(dma(nst_o[:, :], Cst[:, :, DV], slow=True))
                out_dmas.append(dma(ms_o[:, :], mlast[:]))
            else:
                out_dmas.append(dma(cp[seq].rearrange("h d e -> d h e"), Cst[:, :, 0:DV]))
                out_dmas.append(dma(npt[seq], Cst[:, :, DV], slow=True))
                out_dmas.append(dma(mp[seq], mlast[:]))

        if wstate["fused"]:
            pump(UID["wo3"] + 3)
        mark('phase 5: norm2')
        rmsnorm_to_hT(ss1, rs1)

        mark('phase 6: ff1')
        for u in range(16):
            slot = load_unit("f1_%d" % u)
            if u == 0 and TS == 4:
                T3 = 3 * 128
                for sub in range(4):
                    for k in range(KC):
                        mm(ps[sub][:, 0:T3], slot[:, k, sub * 128:(sub + 1) * 128], hT[:, k, 0:T3],
                           start=(k == 0), stop=(k == KC - 1))
                for sub in range(4):
                    for k in range(KC):
                        mm(ps[sub][:, T3:T], slot[:, k, sub * 128:(sub + 1) * 128], hT[:, k, T3:T],
                           start=(k == 0), stop=(k == KC - 1))
            for sub in range(4):
                p_ = ps[sub]
                if not (u == 0 and TS == 4):
                    for k in range(KC):
                        mm(p_[:, 0:T], slot[:, k, sub * 128:(sub + 1) * 128], hT[:, k, 0:T], start=(k == 0), stop=(k == KC - 1))
                t_ = tmpA[cnt["a"] % 2]
                cnt["a"] += 1
                act(t_[:, 0:T], p_[:, 0:T], AF.Relu)
                tt("dve" if sub % 2 == 0 else "pool", hid[:, u * 4 + sub, 0:T], t_[:, 0:T], t_[:, 0:T], ALU.mult)

        mark('phase 7: ff2')
        hnA = tmpA[0][:, :].bitcast(BF16)
        hnB = tmpA[1][:, :].bitcast(BF16)

        xstg_ = xstg_ov if xstg_ov is not None else xstg

        def pro_a(ts):
            xstg = xstg_
            dma(xstg[:, :], nxt(ts))
            memset("pool", ssP2[:, :], 0.0)
            act(hnA[:, :], xstg[:, 0:1024], AF.Square, accum=ssP2[:, 0:1])
            act(hnB[:, :], xstg[:, 1024:2048], AF.Square, accum=ssP2[:, 1:2])
            tt("dve", ssP[:, ts:ts + 1], ssP2[:, 0:1], ssP2[:, 1:2], ALU.add)
            act(rsP[:, ts:ts + 1], ssP[:, ts:ts + 1], AF.Ln, scale=1.0 / D, bias=epsc[:, :])
            act(rsP[:, ts:ts + 1], rsP[:, ts:ts + 1], AF.Exp, scale=-0.5)
            ts_("dve", hnA[:, :], xstg[:, 0:1024], rsP[:, ts:ts + 1], None, ALU.mult)
            ts_("dve", hnB[:, :], xstg[:, 1024:2048], rsP[:, ts:ts + 1], None, ALU.mult)

        def pro_b(ts):
            for half in range(2):
                src_ = hnA if half == 0 else hnB
                pb = ps[4 + half][:, 0:512].bitcast(BF16).rearrange("p (k t) -> p k t", k=8)
                for k8 in range(8):
                    A("pe", lambda e, pb=pb, k8=k8, src_=src_: e.transpose(
                        out=pb[:, k8, :], in_=src_[:, k8 * 128:(k8 + 1) * 128], identity=identb[:, :]),
                      reads=[src_[:, k8 * 128:(k8 + 1) * 128], identb[:, :]], writes=[pb[:, k8, :]])
                cp_("act" if half == 0 else "dve", hT[:, half * 8:(half + 1) * 8, ts * 128:(ts + 1) * 128], pb[:, :, :])

        dma(gf_bc[:, :], gf_d[0:1, :].to_broadcast([128, D]))
        for cs in range(4):
            for q in range(4):
                if nxt is not None and cs >= 2:
                    step = (cs - 2) * 4 + q
                    if step % 2 == 0:
                        pro_a(step // 2)
                    else:
                        pro_b(step // 2)
                slot = load_unit("f2_%d_%d" % (cs, q))
                for ts in range(TS):
                    for k in range(KC):
                        mm(ps[ts][0:PT, :], hid[:, q * 16 + k, ts * 128:ts * 128 + PT], slot[:, k, :],
                           start=(q == 0 and k == 0), stop=(q == 3 and k == KC - 1))
                    if q == 3:
                        xs_ = xt[0:PT, ts, cs * 512:(cs + 1) * 512]
                        tt("dve", xs_, xs_, ps[ts][0:PT, :], ALU.add)

        if nxt is not None:
            prefetch_unit("v0")
            prefetch_unit("v1")
        mark('phase 8: final norm')
        memset("pool", ss1[:, :], 0.0)
        for ts in range(TS):
            act(junk[0:PT, :], xt[0:PT, ts, :], AF.Square, accum=ss1[0:PT, ts:ts + 1])
            act(rs1[0:PT, ts:ts + 1], ss1[0:PT, ts:ts + 1], AF.Ln, scale=1.0 / D, bias=epsc[0:PT, :])
            act(rs1[0:PT, ts:ts + 1], rs1[0:PT, ts:ts + 1], AF.Exp, scale=-0.5)
            stt("dve", xt[0:PT, ts, :], xt[0:PT, ts, :], rs1[0:PT, ts:ts + 1], gf_bc[0:PT, :], ALU.mult, ALU.mult)
            out_dmas.append(dma(ydst(ts), xt[0:PT, ts, :]))

    def xsrc_of(seq, tl):
        t0 = tl * 512
        return lambda ts, seq=seq, t0=t0: xp[seq, t0 + ts * 128:t0 + (ts + 1) * 128, :]

    if fused:
        wstate["fused"] = True
        run_tile(64, lambda ts: xs[:, :], lambda ts: ys[:, :], first=True, last=True, kind="sample", seq=0,
                 nxt=(xsrc_of(0, 0) if pipeline else None), xstg_ov=bview(36864, 4096, F32))
        wstate["fused"] = False
        pump(len(units) - 1)
        S.freeze("wscr")
    mark('prompt tiles')
    tiles = [(seq, tl) for seq in range(n_seq) for tl in range(n_prompt_tiles)]

    for i, (seq, tl) in enumerate(tiles):
        t0 = tl * 512
        nxt = xsrc_of(*tiles[i + 1]) if (pipeline and i + 1 < len(tiles)) else None
        run_tile(512, xsrc_of(seq, tl),
                 lambda ts, seq=seq, t0=t0: yp[seq, t0 + ts * 128:t0 + (ts + 1) * 128, :],
                 first=(tl == 0), last=(tl == n_prompt_tiles - 1), kind="prompt", seq=seq,
                 skip_norm1=(pipeline and (i > 0 or fused)), nxt=nxt)
    if max_ins is not None:
        S.ins = S.ins[:max_ins]
        out_dmas = [k for k, it in enumerate(S.ins) if it["dma"]]
    A("sp", None, extra_deps=out_dmas)
    info = S.emit(st)
    info['marks'] = marks
    st.close()
    return nc, info


_CACHE = {}
NCORES = 8


def kernel(x_prompt, x_sample, state_mlstm_C, state_mlstm_n, state_mlstm_m, w_in, b_gate, g_mh, g_sgu,
           w_sp, b_sp, w_out, g_norm1, g_norm2, w_ff1, w_ff2, g_final):
    f = lambda a: np.ascontiguousarray(np.asarray(a, dtype=np.float32))
    x_prompt = f(x_prompt); x_sample = f(x_sample)
    C0 = f(state_mlstm_C)[0]; n0 = f(state_mlstm_n)[0]; m0 = f(state_mlstm_m)[0]
    w_in0 = f(w_in)[0]; w_out0 = f(w_out)[0]; w_ff10 = f(w_ff1)[0]; w_ff20 = f(w_ff2)[0]
    bg = f(b_gate)[0]
    if "nc" not in _CACHE:
        _CACHE["nc"] = build_program()[0]
    nc = _CACHE["nc"]
    shared = {
        "w_in": w_in0, "w_out": w_out0, "w_ff1": w_ff10, "w_ff2": w_ff20,
        "bgi": np.ascontiguousarray(bg[0:4].reshape(4, 1)), "bgf": np.ascontiguousarray(bg[4:8].reshape(4, 1)),
        "g_mh": f(g_mh)[0].reshape(1, DM), "g_sgu": f(g_sgu)[0].reshape(1, DS),
        "w_sp": f(w_sp)[0], "b_sp": f(b_sp)[0].reshape(1, 512),
        "g1t": np.ascontiguousarray(f(g_norm1)[0].reshape(KC, 128).T),
        "g2t": np.ascontiguousarray(f(g_norm2)[0].reshape(KC, 128).T),
        "g_final": f(g_final).reshape(1, D),
    }
    in_maps = []
    for i in range(NCORES):
        m = dict(shared)
        m["xp"] = x_prompt[2 * i:2 * i + 2]
        m["xs"] = x_sample[i]
        m["c0"] = C0[i]
        m["n0t"] = np.ascontiguousarray(n0[i].T)
        m["m0"] = np.ascontiguousarray(m0[i].reshape(NH, 1))
        in_maps.append(m)
    res = run_bass_kernel_spmd(nc, in_maps, core_ids=list(range(NCORES)))
    R = list(res.results) + [res.results[0]] * (8 - NCORES)
    y_prompt = np.concatenate([R[i]["yp"] for i in range(8)], axis=0)
    y_sample = np.stack([R[i]["ys"] for i in range(8)], axis=0)
    C_prompt = np.concatenate([R[i]["cp"] for i in range(8)], axis=0)[None]
    n_prompt = np.concatenate([np.transpose(R[i]["npt"], (0, 2, 1)) for i in range(8)], axis=0)[None]
    m_prompt = np.concatenate([R[i]["mp"].reshape(2, NH) for i in range(8)], axis=0)[None]
    C_sample = np.stack([R[i]["cs"] for i in range(8)], axis=0)[None]
    n_sample = np.stack([R[i]["nst"].T for i in range(8)], axis=0)[None]
    m_sample = np.stack([R[i]["ms"].reshape(NH) for i in range(8)], axis=0)[None]
    sgu_v = np.stack([R[i]["vs"] for i in range(8)], axis=0)[None]
    outs = (y_prompt, y_sample, C_prompt, n_prompt, m_prompt, C_sample, n_sample, m_sample, sgu_v)
    return tuple(np.ascontiguousarray(o, dtype=np.float32) for o in outs)
```

```python
import numpy as np
from contextlib import ExitStack
import concourse.bass as bass
import concourse.mybir as mybir
from concourse.bass_utils import run_bass_kernel_spmd

F32 = mybir.dt.float32
BF16 = mybir.dt.bfloat16
AF = mybir.ActivationFunctionType
ALU = mybir.AluOpType

D = 2048
KC = 16
NH = 4
DQK = 128
DV = 256
DM = 1024
DS = 1024
DFF = 8192
SEQ = 2048
DEC = 64
EPS = 1e-6
NEG = -30000.0
SEG = 20000
NDMASEM = 24
DSZ = {F32: 4, BF16: 2}


def _dsize(dt):
    return 2 if dt == BF16 else 4


def region(ap):
    pat = ap.ap
    ds = _dsize(ap.dtype)
    off = ap.offset
    sp = str(ap.space)
    if "DRAM" in sp:
        span = 1
        for (s, c) in pat:
            span += (c - 1) * s
        return (ap.name, 0, 1, off * ds, (off + span) * ds)
    if "PSUM" in sp:
        return (ap.name, 0, 128, 0, 2048)
    ps, pc = pat[0]
    if ps == 0:
        ps = 1 << 40
    p0 = off // ps
    lo = off % ps
    fr = [(s_, c_) for (s_, c_) in pat[1:] if c_ > 1]
    if len(fr) == 2 and fr[1][0] == 1 and 1 < fr[0][1] <= 16 and fr[0][0] > fr[1][1]:
        s1, c1 = fr[0]
        c2 = fr[1][1]
        return [(ap.name, p0, p0 + pc, (lo + i * s1) * ds, (lo + i * s1 + c2) * ds) for i in range(c1)]
    span = 1
    for (s, c) in pat[1:]:
        span += (c - 1) * s
    return (ap.name, p0, p0 + pc, lo * ds, (lo + span) * ds)


def regions(aps):
    out = []
    for a in aps:
        r = region(a)
        if isinstance(r, list):
            out.extend(r)
        else:
            out.append(r)
    return out


class Sched:
    def __init__(self, nc):
        self.nc = nc
        self.ins = []
        self.acc = {}
        self.frozen = set()
        self.eng_obj = {"pe": nc.tensor, "act": nc.scalar, "dve": nc.vector,
                        "pool": nc.gpsimd, "sp": nc.sync}

    def freeze(self, name):
        self.frozen.add(name)

    def add(self, eng, fn, reads=(), writes=(), dma=False, extra_deps=()):
        idx = len(self.ins)
        deps = set(extra_deps)
        rr = regions(reads)
        ww = regions(writes)
        for (nm, p0, p1, lo, hi) in rr:
            d = self.acc.get(nm)
            if d:
                psum = nm.startswith("ps")
                for (k, j) in d.items():
                    if (k[5] or (psum and k[0] != eng)) and k[1] < p1 and p0 < k[2] and k[3] < hi and lo < k[4]:
                        deps.add(j)
        for (nm, p0, p1, lo, hi) in ww:
            d = self.acc.get(nm)
            if d:
                for (k, j) in d.items():
                    if k[1] < p1 and p0 < k[2] and k[3] < hi and lo < k[4]:
                        deps.add(j)
        tag = idx if dma else eng
        for (nm, p0, p1, lo, hi) in rr:
            if nm in self.frozen:
                continue
            self.acc.setdefault(nm, {})[(tag, p0, p1, lo, hi, False)] = idx
        for (nm, p0, p1, lo, hi) in ww:
            d = self.acc.setdefault(nm, {})
            dead = [k for k in d if k[1] >= p0 and k[2] <= p1 and k[3] >= lo and k[4] <= hi]
            for k in dead:
                del d[k]
            d[(tag, p0, p1, lo, hi, True)] = idx
        deps.discard(idx)
        self.ins.append(dict(eng=eng, fn=fn, deps=deps, dma=dma))
        return idx

    def emit(self, stack):
        nc = self.nc
        ins = self.ins
        n = len(ins)

        def skip(dd, it):
            return (not dd["dma"]) and (not it["dma"]) and dd["eng"] == "pe" and it["eng"] == "pe"

        needed = [False] * n
        for it in ins:
            for d in it["deps"]:
                if not skip(ins[d], it):
                    needed[d] = True
        sems = {}

        def get_sem(key):
            if key not in sems:
                sems[key] = stack.enter_context(nc.semaphore("s%d" % len(sems)))
            return sems[key]

        sig = [None] * n
        eng_cnt = {}
        ndma = 0
        dma_prev = {}
        for k, it in enumerate(ins):
            if it["fn"] is None:
                continue
            if it["dma"]:
                slot = ndma % NDMASEM
                gen = ndma // NDMASEM
                ndma += 1
                sig[k] = (("dma", slot), 16 * (gen + 1))
                it["dma_prev"] = (("dma", slot), 16 * gen) if gen > 0 else None
            elif needed[k]:
                c = eng_cnt.get(it["eng"], 0)
                eng_cnt[it["eng"]] = c + 1
                sig[k] = (("eng", it["eng"], c // SEG), (c % SEG) + 1)
        waited = {e: {} for e in self.eng_obj}
        nwaits = 0
        for k, it in enumerate(ins):
            e = it["eng"]
            eo = self.eng_obj[e]
            need = {}
            for d in it["deps"]:
                if skip(ins[d], it):
                    continue
                skey, val = sig[d]
                if need.get(skey, 0) < val:
                    need[skey] = val
            if it.get("dma_prev"):
                skey, val = it["dma_prev"]
                if need.get(skey, 0) < val:
                    need[skey] = val
            for skey, val in need.items():
                if waited[e].get(skey, 0) >= val:
                    continue
                eo.wait_ge(get_sem(skey), val)
                waited[e][skey] = val
                nwaits += 1
            if it["fn"] is None:
                continue
            r = it["fn"](eo)
            if sig[k] is not None:
                skey, val = sig[k]
                r.then_inc(get_sem(skey), 16 if it["dma"] else 1)
        return dict(n=n, nwaits=nwaits, nsems=len(sems), ndma=ndma, eng_cnt=eng_cnt)


def build_program(n_prompt_tiles=4, do_sample=True, n_seq=2, max_ins=None, pipeline=True):
    nc = bass.Bass("TRN2", target_bir_lowering=False)
    dram = lambda n, s, dt, k: nc.dram_tensor(n, s, dt, kind=k).ap()
    xp = dram("xp", [2, SEQ, D], F32, "ExternalInput")
    xs = dram("xs", [DEC, D], F32, "ExternalInput")
    c0 = dram("c0", [NH, DQK, DV], F32, "ExternalInput")
    n0t = dram("n0t", [DQK, NH], F32, "ExternalInput")
    m0 = dram("m0", [NH, 1], F32, "ExternalInput")
    w_in = dram("w_in", [D, 5128], F32, "ExternalInput")
    w_out = dram("w_out", [D, D], F32, "ExternalInput")
    w_ff1 = dram("w_ff1", [D, DFF], F32, "ExternalInput")
    w_ff2 = dram("w_ff2", [DFF, D], F32, "ExternalInput")
    bgi_d = dram("bgi", [NH, 1], F32, "ExternalInput")
    bgf_d = dram("bgf", [NH, 1], F32, "ExternalInput")
    gmh_d = dram("g_mh", [1, DM], F32, "ExternalInput")
    gsgu_d = dram("g_sgu", [1, DS], F32, "ExternalInput")
    wsp_d = dram("w_sp", [4, 128, 128], F32, "ExternalInput")
    bsp_d = dram("b_sp", [1, 512], F32, "ExternalInput")
    g1t_d = dram("g1t", [128, KC], F32, "ExternalInput")
    g2t_d = dram("g2t", [128, KC], F32, "ExternalInput")
    gf_d = dram("g_final", [1, D], F32, "ExternalInput")

    yp = dram("yp", [2, SEQ, D], F32, "ExternalOutput")
    ys = dram("ys", [DEC, D], F32, "ExternalOutput")
    cp = dram("cp", [2, NH, DQK, DV], F32, "ExternalOutput")
    npt = dram("npt", [2, DQK, NH], F32, "ExternalOutput")
    mp = dram("mp", [2, NH, 1], F32, "ExternalOutput")
    cs_o = dram("cs", [NH, DQK, DV], F32, "ExternalOutput")
    nst_o = dram("nst", [DQK, NH], F32, "ExternalOutput")
    ms_o = dram("ms", [NH, 1], F32, "ExternalOutput")
    vs_o = dram("vs", [DEC, DS], F32, "ExternalOutput")

    NUNIT = 46
    wscr = dram("wscr", [NUNIT, 128, 8192], BF16, "Internal")

    st = ExitStack()
    E = st.enter_context
    sb = lambda n, s, dt: E(nc.sbuf_tensor(n, s, dt))
    xt = sb("xt", [128, 4, D], F32)
    hT = sb("hT", [128, KC, 512], BF16)
    BIGN = 42000
    big = sb("big", [128, BIGN], BF16)
    wr = [sb("wr%d" % i, [128, KC, 512], BF16) for i in range(3)]
    wg = sb("wg", [128, KC, 8], BF16)
    identf = sb("identf", [128, 128], F32)
    identb = sb("identb", [128, 128], BF16)
    sel = sb("sel", [4, 4, 128], F32)
    maskneg = sb("maskneg", [128, 128], BF16)
    ones_bf = sb("ones_bf", [128, 1], BF16)
    onesf = sb("onesf", [1, 128], F32)
    epsc = sb("epsc", [128, 1], F32)
    onec = sb("onec", [128, 1], F32)
    wspT = sb("wspT", [128, 4, 128], BF16)
    bsp_hi = sb("bsp_hi", [1, 512], BF16)
    bsp_lo = sb("bsp_lo", [1, 512], BF16)
    ones_b1 = sb("ones_b1", [1, 128], BF16)
    gsgu_bc = sb("gsgu_bc", [128, DS], F32)
    gmh_bc = sb("gmh_bc", [128, DM], F32)
    g1t = sb("g1t_sb", [128, KC], F32)
    g2t = sb("g2t_sb", [128, KC], F32)
    bgi = sb("bgi_sb", [4, 1], F32)
    bgf = sb("bgf_sb", [4, 1], F32)
    Cst = sb("Cst", [128, NH, DV + 1], F32)
    Cbf = sb("Cbf", [128, NH, DV + 1], BF16)
    denS = sb("denS", [128, 2, NH], F32)
    Bcar = sb("Bcar", [4, 1], F32)
    Mcar = sb("Mcar", [4, 1], F32)
    mlast = sb("mlast", [4, 1], F32)
    Mb = sb("Mb", [4, 8], F32)
    Me = sb("Me", [4, 8], F32)
    cols = sb("cols", [128, 4, 8], F32)
    dec = sb("dec", [128, NH, 8], F32)
    ss1 = sb("ss1", [128, 4], F32)
    rs1 = sb("rs1", [128, 4], F32)
    ssz = sb("ssz", [128, 4, 2], F32)
    rz = sb("rz", [128, 4], F32)
    ssn = sb("ssn", [128, 4], F32)
    ssP2 = sb("ssP2", [128, 2], F32)
    ssP = sb("ssP", [128, 4], F32)
    rsP = sb("rsP", [128, 4], F32)
    sm_a = sb("sm_a", [128, 4], F32)
    sm_b = sb("sm_b", [128, 4], F32)
    sm_f = sb("sm_f", [128, 4], F32)
    tmpA = [sb("tmpA%d" % i, [128, 512], F32) for i in range(2)]
    junk2 = sb("junk2", [128, 512], BF16)
    ps = [E(nc.psum_tensor("ps%d" % i, [128, 512], F32)) for i in range(8)]

    def bview(lo, n, dt=BF16):
        v = big[:, lo:lo + n]
        return v.bitcast(F32) if dt == F32 else v

    junk = bview(0, 2048)
    hn = [bview(2048, 2048), bview(4096, 2048)]
    qT = bview(0, 2048).rearrange("p (h t) -> p h t", h=4)
    qh = bview(2048, 2048).rearrange("p (h t) -> p h t", h=4)
    kT = bview(4096, 2048).rearrange("p (h t) -> p h t", h=4)
    kw = bview(6144, 2048).rearrange("p (s h d) -> p s h d", s=4, h=4)
    vv = bview(8192, 4112).rearrange("p (s h d) -> p s h d", s=4, h=4)
    gso = bview(12304, 4096).rearrange("p (s c) -> p s c", s=4)
    uTg = bview(16400, 4096).rearrange("p (c t) -> p c t", c=8)
    zs = bview(20496, 4096).rearrange("p (s c) -> p s c", s=4)
    ym = bview(24592, 4096).rearrange("p (s c) -> p s c", s=4)
    sT = bview(28688, 2048).rearrange("p (h j t) -> p h j t", h=4, j=4)
    DT = None
    Wbc = [bview(30736 + 1024 * i, 1024, F32) for i in range(2)]
    gzS = bview(32784, 2048, F32)
    wspS = [bview(34832 + 512 * i, 512).rearrange("p (g t) -> p g t", g=4) for i in range(2)]
    rows = [bview(35856 + 1024 * i, 1024, F32) for i in range(6)]
    gf_bc = bview(37904, 4096, F32)
    xstg = bview(32768, 4096, F32)
    hid = bview(0, 32768).rearrange("p (c t) -> p c t", c=64)
    wstage = [bview(16384 * i, 16384, F32).rearrange("p (k c) -> p k c", k=KC) for i in range(2)]

    LM = dict(junk=junk, hn=hn, qT=qT, qh=qh, kT=kT, kw=kw, vv=vv, gso=gso, uTg=uTg, zs=zs, ym=ym, sT=sT, DT=DT,
              Wbc=Wbc, gzS=gzS, wspS=wspS, rows=rows, gf_bc=gf_bc, hid=hid)
    _o = [32768]

    def sv(n, dt=BF16):
        v = bview(_o[0], n, dt)
        _o[0] += n
        return v
    _hn0 = xt[:, 2, 1024:2048].bitcast(BF16)
    LS = dict(
        qT=sv(256).rearrange("p (h t) -> p h t", h=4), qh=sv(256).rearrange("p (h t) -> p h t", h=4),
        kT=sv(256).rearrange("p (h t) -> p h t", h=4),
        kw=sv(512).rearrange("p (s h d) -> p s h d", s=1, h=4),
        vv=sv(1028).rearrange("p (s h d) -> p s h d", s=1, h=4),
        gso=sv(1024).rearrange("p (s c) -> p s c", s=1),
        uTg=sv(512).rearrange("p (c t) -> p c t", c=8),
        zs=sv(1024).rearrange("p (s c) -> p s c", s=1),
        ym=sv(1024).rearrange("p (s c) -> p s c", s=1),
        sT=sv(256).rearrange("p (h j t) -> p h j t", h=4, j=1),
        DT=[sv(128, F32).rearrange("p (j t) -> p j t", j=1) for _ in range(2)],
        Wbc=[sv(128, F32) for _ in range(2)],
        wspS=[sv(512).rearrange("p (g t) -> p g t", g=4) for _ in range(2)],
        rows=[sv(128, F32) for _ in range(6)],
        hid=bview(32768, 4096).rearrange("p (c t) -> p c t", c=64),
        gf_bc=xt[:, 1, :], junk=xt[:, 2, 0:1024].bitcast(BF16), hn=[_hn0, _hn0], gzS=xt[:, 3, 0:1024],
    )
    assert _o[0] <= BIGN
    S = Sched(nc)
    A = S.add
    marks = []
    mark = lambda l: marks.append((l, len(S.ins)))

    def dma(out, in_, eng="sp", slow=False):
        if slow:
            return A(eng, lambda e: e.dma_start(out=out, in_=in_, allow_slow_non_contiguous=True),
                     reads=[in_], writes=[out], dma=True)
        return A(eng, lambda e: e.dma_start(out=out, in_=in_), reads=[in_], writes=[out], dma=True)

    def mm(out, lhsT, rhs, start=True, stop=True):
        return A("pe", lambda e: e.matmul(out, lhsT=lhsT, rhs=rhs, start=start, stop=stop),
                 reads=[lhsT, rhs], writes=[out])

    def act(out, in_, func, scale=1.0, bias=None, accum=None, extra_r=()):
        rd = [in_] + list(extra_r)
        wr_ = [out]
        kw_ = dict(out=out, in_=in_, func=func)
        if not isinstance(scale, float):
            rd.append(scale)
        kw_["scale"] = scale
        if bias is not None:
            kw_["bias"] = bias
            rd.append(bias)
        if accum is not None:
            kw_["accum_out"] = accum
            wr_.append(accum)
        return A("act", lambda e: e.activation(**kw_), reads=rd, writes=wr_)

    def tt(eng, out, in0, in1, op):
        return A(eng, lambda e: e.tensor_tensor(out=out, in0=in0, in1=in1, op=op), reads=[in0, in1], writes=[out])

    def ts_(eng, out, in0, s1, s2, op0, op1=None):
        rd = [in0] + [s for s in (s1, s2) if s is not None and not isinstance(s, float)]
        if op1 is None:
            s2, op1 = 0.0, ALU.add
        return A(eng, lambda e: e.tensor_scalar(out=out, in0=in0, scalar1=s1, scalar2=s2, op0=op0, op1=op1),
                 reads=rd, writes=[out])

    def stt(eng, out, in0, scalar, in1, op0, op1):
        rd = [in0, in1] + ([] if isinstance(scalar, float) else [scalar])
        return A(eng, lambda e: e.scalar_tensor_tensor(out=out, in0=in0, scalar=scalar, in1=in1, op0=op0, op1=op1),
                 reads=rd, writes=[out])

    def cp_(eng, out, in_):
        if eng == "act":
            return A("act", lambda e: e.copy(out=out, in_=in_), reads=[in_], writes=[out])
        return A(eng, lambda e: e.tensor_copy(out=out, in_=in_), reads=[in_], writes=[out])

    def memset(eng, ap, val):
        return A(eng, lambda e: e.memset(ap, val), writes=[ap])

    def asel(out, in_, pattern, op, fill, base, cm):
        return A("pool", lambda e: e.affine_select(out=out, in_=in_, pattern=pattern, compare_op=op, fill=fill,
                                                   base=base, channel_multiplier=cm), reads=[in_], writes=[out])

    out_dmas = []

    memset("pool", identf[:], 0.0)
    asel(identf[:], identf[:], [[-1, 128]], ALU.not_equal, 1.0, 0, 1)
    cp_("dve", identb[:], identf[:])
    memset("pool", sel[:], 0.0)
    asel(sel[:], sel[:], [[-1, 4], [0, 128]], ALU.not_equal, 1.0, 0, 1)
    maskf = bview(4096, 256, F32)
    memset("pool", maskf, 0.0)
    asel(maskf, maskf, [[1, 128]], ALU.is_ge, NEG, 0, -1)
    cp_("dve", maskneg[:], maskf)
    memset("pool", ones_bf[:], 1.0)
    memset("pool", onesf[:], 1.0)
    memset("pool", epsc[:], EPS)
    memset("pool", onec[:], 1.0)
    dma(bgi[:], bgi_d[:, :])
    dma(bgf[:], bgf_d[:, :])
    dma(g1t[:], g1t_d[:, :])
    dma(g2t[:], g2t_d[:, :])
    bsp = bview(8192, 1024, F32)[0:1, :]
    bsp_t = bview(9216, 1024, F32)[0:1, :]
    dma(bsp, bsp_d[:, :])
    cp_("dve", bsp_hi[:], bsp)
    cp_("dve", bsp_t, bsp_hi[:])
    tt("dve", bsp_t, bsp, bsp_t, ALU.subtract)
    cp_("dve", bsp_lo[:], bsp_t)
    memset("pool", ones_b1[:], 1.0)
    dma(gsgu_bc[:], gsgu_d[0:1, :].to_broadcast([128, DS]))
    dma(gmh_bc[:], gmh_d[0:1, :].to_broadcast([128, DM]))
    ts_("dve", gmh_bc[:], gmh_bc[:], 0.5, None, ALU.mult)
    wspf = bview(0, 1024, F32).rearrange("p (g s) -> p g s", g=4)
    wspb = bview(1024, 512).rearrange("p (g s) -> p g s", g=4)
    dma(wspf, wsp_d.rearrange("g t s -> t g s"))
    asel(wspf, wspf, [[0, 4], [-1, 128]], ALU.is_ge, 0.0, 0, 1)
    cp_("dve", wspb, wspf)
    pst = ps[0][:, 0:256].bitcast(BF16).rearrange("p (g t) -> p g t", g=4)
    for g in range(4):
        A("pe", lambda e, g=g: e.transpose(out=pst[:, g, :], in_=wspb[:, g, :], identity=identb[:]),
          reads=[wspb[:, g, :], identb[:]], writes=[pst[:, g, :]])
    cp_("dve", wspT[:], pst)

    mark('prepass')
    units = []
    win_v = w_in.rearrange("(k p) c -> p k c", p=128)
    col0 = {"q": 0, "k": 512, "v0": 1024, "v1": 1536, "o0": 2048, "o1": 2560,
            "u0": 3080, "u1": 3592, "z0": 4104, "z1": 4616}
    UID = {}
    for nm in ["v0", "v1", "o0", "o1", "z0", "z1", "u0", "u1", "k", "q"]:
        UID[nm] = len(units)
        units.append((win_v[:, :, col0[nm]:col0[nm] + 512], g1t))
    wout_v = w_out.rearrange("(k p) c -> p k c", p=128)
    for c in range(4):
        UID["wo%d" % c] = len(units)
        units.append((wout_v[:, :, c * 512:(c + 1) * 512], None))
    wff1_v = w_ff1.rearrange("(k p) c -> p k c", p=128)
    for c in range(16):
        UID["f1_%d" % c] = len(units)
        units.append((wff1_v[:, :, c * 512:(c + 1) * 512], g2t))
    wff2_v = w_ff2.rearrange("(q k p) c -> q p k c", q=4, p=128)
    for c in range(4):
        for q in range(4):
            UID["f2_%d_%d" % (c, q)] = len(units)
            units.append((wff2_v[q][:, :, c * 512:(c + 1) * 512], None))
    assert len(units) == NUNIT
    gst = bview(32768, 256, F32).rearrange("p (k c) -> p k c", k=KC)
    dma(gst, win_v[:, :, 3072:3080])
    tt("dve", wg[:], gst, g1t[:].unsqueeze(2).to_broadcast([128, KC, 8]), ALU.mult)
    def stage_load(u):
        src = units[u][0]
        stg = wstage[u % 2]
        for q4 in range(4):
            dma(stg[:, q4 * 4:(q4 + 1) * 4, :], src[:, q4 * 4:(q4 + 1) * 4, :])

    pre = dict(cast=0)

    def do_cast(u):
        src, gt = units[u]
        stg = wstage[u % 2]
        if u == 0:
            stage_load(0)
        if u + 1 < len(units):
            stage_load(u + 1)
        dst = wr[u % 3]
        if gt is None:
            half = 8
            cp_("dve", dst[:, 0:half, :], stg[:, 0:half, :])
            cp_("pool", dst[:, half:KC, :], stg[:, half:KC, :])
        else:
            for k in range(KC):
                eng = "dve" if k % 4 != 3 else "pool"
                ts_(eng, dst[:, k, :], stg[:, k, :], gt[:, k:k + 1], None, ALU.mult)
        dma(wscr[u], dst[:].rearrange("p k c -> p (k c)"))

    def pump(upto):
        while pre["cast"] <= min(upto, len(units) - 1):
            do_cast(pre["cast"])
            pre["cast"] += 1

    fused = bool(do_sample)
    if not fused:
        pump(len(units) - 1)
        S.freeze("wscr")
    mark('tiles')

    wstate = dict(n=0, fused=False, pf=[])

    def load_unit(name):
        mark('unit ' + name)
        if wstate["fused"]:
            u = UID[name]
            pump(u + 2)
            wstate["n"] = u + 1
            return wr[u % 3]
        pf = wstate["pf"]
        if pf and pf[0][0] == name:
            return pf.pop(0)[1]
        slot = wr[wstate["n"] % 3]
        wstate["n"] += 1
        dma(slot[:].rearrange("p k c -> p (k c)"), wscr[UID[name]])
        return slot

    def prefetch_unit(name):
        slot = wr[wstate["n"] % 3]
        wstate["n"] += 1
        dma(slot[:].rearrange("p k c -> p (k c)"), wscr[UID[name]])
        wstate["pf"].append((name, slot))

    def run_tile(T, xsrc, ydst, first, last, kind, seq, skip_norm1=False, nxt=None, xstg_ov=None):
        TS = (T + 127) // 128
        PT = min(T, 128)
        LC = min(T, 128)
        NCH = T // LC
        NJ = TS
        sample = kind == "sample"
        L = LS if (sample and wstate["fused"]) else LM
        junk, hn, qT, qh, kT, kw, vv, gso, uTg, zs, ym, sT, DT, Wbc, gzS, wspS, rows, gf_bc, hid = [
            L[k_] for k_ in ("junk", "hn", "qT", "qh", "kT", "kw", "vv", "gso", "uTg", "zs", "ym", "sT", "DT", "Wbc",
                             "gzS", "wspS", "rows", "gf_bc", "hid")]
        def load_x():
            for ts in range(TS):
                dma(xt[0:PT, ts, :], xsrc(ts))
        if not skip_norm1:
            load_x()

        def rmsnorm_to_hT(ssbuf, rsbuf, defer_last=False):
            def pe_part(ts, h_, banks):
                for half in range(2):
                    pb = banks[half][:, 0:512].bitcast(BF16).rearrange("p (k t) -> p k t", k=8)
                    for k8 in range(8):
                        k = half * 8 + k8
                        A("pe", lambda e, pb=pb, k8=k8, k=k, h_=h_: e.transpose(
                            out=pb[:, k8, 0:PT], in_=h_[0:PT, k * 128:(k + 1) * 128], identity=identb[0:PT, 0:PT]),
                          reads=[h_[0:PT, k * 128:(k + 1) * 128], identb[0:PT, 0:PT]], writes=[pb[:, k8, 0:PT]])
                    cp_("act" if half == 0 else "dve", hT[:, half * 8:(half + 1) * 8, ts * 128:ts * 128 + PT],
                        pb[:, :, 0:PT])

            deferred = None
            memset("pool", ssbuf[:, :], 0.0)
            for ts in range(TS):
                act(junk[0:PT, :], xt[0:PT, ts, :], AF.Square, accum=ssbuf[0:PT, ts:ts + 1])
                act(rsbuf[0:PT, ts:ts + 1], ssbuf[0:PT, ts:ts + 1], AF.Ln, scale=1.0 / D, bias=epsc[0:PT, :])
                act(rsbuf[0:PT, ts:ts + 1], rsbuf[0:PT, ts:ts + 1], AF.Exp, scale=-0.5)
                h_ = hn[ts % 2]
                ts_("dve", h_[0:PT, :], xt[0:PT, ts, :], rsbuf[0:PT, ts:ts + 1], None, ALU.mult)
                if defer_last and ts == TS - 1:
                    deferred = (lambda ts=ts, h_=h_: pe_part(ts, h_, [ps[4], ps[5]]))
                else:
                    pe_part(ts, h_, [ps[(ts * 2) % 4], ps[(ts * 2 + 1) % 4]])
            return deferred

        mark('phase 1: norm1')
        if not skip_norm1:
            rmsnorm_to_hT(ss1, rs1)

        mark('state init')
        if first:
            if sample:
                dma(Cst[:, :, 0:DV], c0.rearrange("h d e -> d h e"))
                dma(Cst[:, :, DV], n0t[:, :], slow=True)
                dma(Mcar[:], m0[:, :])
            else:
                memset("pool", Cst[:], 0.0)
                memset("pool", Mcar[:], 0.0)
            memset("pool", Bcar[:], 0.0)
            cp_("pool", Cbf[:], Cst[:])

        mark('phase 2: input projection')
        def do_gates():
            for k in range(KC):
                mm(ps[4][0:4, 0:T], wg[:, k, 0:4], hT[:, k, 0:T], start=(k == 0), stop=(k == KC - 1))
            for k in range(KC):
                mm(ps[5][0:4, 0:T], wg[:, k, 4:8], hT[:, k, 0:T], start=(k == 0), stop=(k == KC - 1))
            r0, r1, r2, r3, r4, rzero = [r[0:4, 0:T] for r in rows]
            memset("pool", rzero, 0.0)
            ts_("dve", r0, ps[4][0:4, 0:T], bgi[:, 0:1], None, ALU.add)
            ts_("dve", r1, ps[5][0:4, 0:T], bgf[:, 0:1], None, ALU.add)
            act(r2, r1, AF.Abs)
            act(r2, r2, AF.Exp, scale=-1.0)
            act(r2, r2, AF.Ln, bias=onec[0:4, :])
            ts_("dve", r1, r1, 0.0, None, ALU.min)
            tt("dve", r1, r1, r2, ALU.subtract)
            A("dve", lambda e: e.tensor_tensor_scan(out=r2, data0=r1, data1=rzero, initial=Bcar[:, 0:1],
                                                    op0=ALU.add, op1=ALU.add),
              reads=[r1, rzero, Bcar[:, 0:1]], writes=[r2])
            tt("dve", r0, r0, r2, ALU.subtract)
            A("dve", lambda e: e.tensor_tensor_scan(out=r1, data0=r0, data1=r0, initial=Mcar[:, 0:1],
                                                    op0=ALU.max, op1=ALU.max),
              reads=[r0, Mcar[:, 0:1]], writes=[r1])
            tt("dve", r3, r2, r1, ALU.add)
            cp_("dve", mlast[:, 0:1], r3[:, T - 1:T])
            cp_("dve", Mb[:, 0:1], Mcar[:, 0:1])
            if NCH > 1:
                cp_("dve", Mb[:, 1:NCH], r1.rearrange("p (c t) -> p c t", t=LC)[:, 0:NCH - 1, LC - 1])
            cp_("dve", Me[:, 0:NCH], r1.rearrange("p (c t) -> p c t", t=LC)[:, :, LC - 1])
            cp_("dve", Bcar[:, 0:1], r2[:, T - 1:T])
            cp_("dve", Mcar[:, 0:1], r1[:, T - 1:T])
            act(r3, r3, AF.Exp, scale=-1.0)
            r1c = r1.rearrange("p (c t) -> p c t", t=LC)
            tt("dve", r2.rearrange("p (c t) -> p c t", t=LC), Mb[:, 0:NCH].unsqueeze(2).to_broadcast([4, NCH, LC]),
               r1c, ALU.subtract)
            act(r2, r2, AF.Exp)
            tt("dve", r4.rearrange("p (c t) -> p c t", t=LC), r0.rearrange("p (c t) -> p c t", t=LC),
               Me[:, 0:NCH].unsqueeze(2).to_broadcast([4, NCH, LC]), ALU.subtract)
            act(r4, r4, AF.Exp)
            ts_("dve", r1, r1, -1.0, None, ALU.mult)
            gt_r, negM_r, wi_r, em_r, ws_r = r0, r1, r2, r3, r4
            return r0, r1, r2, r3, r4

        def do_cols(ws_r, em_r):
            pcol = ps[6][:, 0:32].rearrange("p (s c) -> p s c", s=4)
            for ts in range(TS):
                mm(pcol[0:PT, ts, 0:4], ws_r[:, ts * 128:ts * 128 + PT], identf[0:4, 0:4])
                mm(pcol[0:PT, ts, 4:8], em_r[:, ts * 128:ts * 128 + PT], identf[0:4, 0:4])
            cp_("dve", cols[0:PT, 0:TS, :], pcol[0:PT, 0:TS, :])

        def tok_major(unit, evac):
            slot = load_unit(unit)
            for ts in range(TS):
                for k in range(KC):
                    mm(ps[ts][0:PT, :], hT[:, k, ts * 128:ts * 128 + PT], slot[:, k, :],
                       start=(k == 0), stop=(k == KC - 1))
                evac(ts, ps[ts])

        def feat_major(unit, evac):
            slot = load_unit(unit)
            for sub in range(4):
                p_ = ps[sub]
                for k in range(KC):
                    mm(p_[:, 0:T], slot[:, k, sub * 128:(sub + 1) * 128], hT[:, k, 0:T],
                       start=(k == 0), stop=(k == KC - 1))
                evac(sub, p_)

        cnt = dict(a=0)

        def ev_v(sl):
            def f(ts, p_):
                eng = "act" if (ts % 2 == 0) else "dve"
                cp_(eng, vv[0:PT, ts, 2 * sl:2 * sl + 2, 0:DV], p_[0:PT, :].rearrange("p (h d) -> p h d", h=2))
            return f

        def ev_o(sl):
            def f(ts, p_):
                t_ = tmpA[cnt["a"] % 2]
                cnt["a"] += 1
                act(t_[0:PT, :], p_[0:PT, :], AF.Tanh, scale=0.5)
                stt("dve", gso[0:PT, ts, sl * 512:(sl + 1) * 512], t_[0:PT, :], 1.0,
                    gmh_bc[0:PT, sl * 512:(sl + 1) * 512], ALU.add, ALU.mult)
            return f

        def ev_z(sl):
            def f(ts, p_):
                t_ = tmpA[cnt["a"] % 2]
                cnt["a"] += 1
                act(t_[0:PT, :], p_[0:PT, :], AF.Gelu)
                tt("dve", zs[0:PT, ts, sl * 512:(sl + 1) * 512], t_[0:PT, :],
                   gsgu_bc[0:PT, sl * 512:(sl + 1) * 512], ALU.mult)
                act(junk2[0:PT, :], t_[0:PT, :], AF.Square, accum=ssz[0:PT, ts, sl:sl + 1])
                if sample:
                    cp_("pool", gzS[0:PT, sl * 512:(sl + 1) * 512], t_[0:PT, :])
            return f

        def ev_u(sl):
            def f(sub, p_):
                act(uTg[:, sl * 4 + sub, 0:T], p_[:, 0:T], AF.Gelu)
            return f

        def ev_kT(h, p_):
            act(kT[:, h, 0:T], p_[:, 0:T], AF.Copy, scale=float(DQK) ** -0.5)

        def ev_kw(ts, p_):
            for h in range(NH):
                ts_("dve", kw[0:PT, ts, h, :], p_[0:PT, h * 128:(h + 1) * 128], cols[0:PT, ts, h:h + 1],
                    float(DQK) ** -0.5, ALU.mult, ALU.mult)

        memset("pool", ssz[:], 0.0)
        tok_major("v0", ev_v(0))
        tok_major("v1", ev_v(1))
        gt_r, negM_r, wi_r, em_r, ws_r = do_gates()
        tok_major("o0", ev_o(0))
        tok_major("o1", ev_o(1))
        if skip_norm1:
            load_x()
        tok_major("z0", ev_z(0))
        tok_major("z1", ev_z(1))
        do_cols(ws_r, em_r)
        feat_major("u0", ev_u(0))
        feat_major("u1", ev_u(1))
        slot = load_unit("k")
        for ts in range(TS):
            for k in range(KC):
                mm(ps[ts][0:PT, :], hT[:, k, ts * 128:ts * 128 + PT], slot[:, k, :], start=(k == 0), stop=(k == KC - 1))
            ev_kw(ts, ps[ts])
            act(ym[0:PT, ts, 0:512], ps[ts][0:PT, :], AF.Copy, scale=float(DQK) ** -0.5)
            pbk = ps[4 + (ts % 2)][:, 0:256].bitcast(BF16).rearrange("p (h t) -> p h t", h=4)
            for h in range(NH):
                A("pe", lambda e, pbk=pbk, h=h, ts=ts: e.transpose(out=pbk[:, h, 0:PT], in_=ym[0:PT, ts, h * 128:(h + 1) * 128],
                                                                     identity=identb[0:PT, 0:PT]),
                  reads=[ym[0:PT, ts, h * 128:(h + 1) * 128], identb[0:PT, 0:PT]], writes=[pbk[:, h, 0:PT]])
            cp_("act" if ts % 2 == 0 else "dve", kT[:, 0:NH, ts * 128:ts * 128 + PT], pbk[:, :, 0:PT])
        for h in range(NH):
            mm(ps[4 + h][:, 0:T], sel[0:4, h, :], wi_r)
        def pre_phase(h):
            P_ = ps[h][:, :].rearrange("p (j t) -> p j t", j=4)
            Sc = ps[4 + h][:, :].rearrange("p (j t) -> p j t", j=4)
            mm(P_[0:PT, 0:NCH, 0:LC], sel[0:4, h, 0:PT], negM_r.rearrange("p (j t) -> p j t", t=LC), start=True, stop=False)
            for j in range(NCH):
                mm(P_[0:PT, j, 0:LC], gt_r[:, j * LC:(j + 1) * LC], sel[0:4, h, 0:LC], start=False, stop=False)
            mm(P_[0:PT, 0:NCH, 0:LC], identb[0:PT, 0:PT],
               maskneg[0:PT, 0:LC].unsqueeze(1).to_broadcast([PT, NCH, LC]), start=False, stop=True)
            for j in range(NCH):
                mm(Sc[0:PT, j, 0:LC], kT[:, h, j * LC:(j + 1) * LC], qT[:, h, j * LC:(j + 1) * LC])
            dt_ = tmpA[h % 2][:, :].rearrange("p (j t) -> p j t", j=4)
            act(dt_[0:PT, 0:NCH, 0:LC], P_[0:PT, 0:NCH, 0:LC], AF.Exp)
            tt("dve", sT[0:PT, h, 0:NCH, 0:LC], Sc[0:PT, 0:NCH, 0:LC], dt_[0:PT, 0:NCH, 0:LC], ALU.mult)

        slot = load_unit("q")
        for h in range(NH):
            for k in range(KC):
                mm(ps[h][:, 0:T], slot[:, k, h * 128:(h + 1) * 128], hT[:, k, 0:T], start=(k == 0), stop=(k == KC - 1))
            wb = Wbc[h % 2]
            cp_("act", wb[:, 0:T], ps[4 + h][:, 0:T])
            cp_("dve", dec[:, h, 0:NCH], wb[:, 0:T].rearrange("p (c t) -> p c t", t=LC)[:, :, LC - 1])
            cp_("act", qT[:, h, 0:T], ps[h][:, 0:T])
            tt("dve", qh[:, h, 0:T], ps[h][:, 0:T], wb[:, 0:T], ALU.mult)
            if h >= 1:
                pre_phase(h - 1)
        pre_phase(NH - 1)

        mark('phase 3: mixers')
        if wstate["fused"]:
            pump(UID["q"] + 3)
        tt("dve", rz[0:PT, 0:TS], ssz[0:PT, 0:TS, 0], ssz[0:PT, 0:TS, 1], ALU.add)
        act(rz[0:PT, 0:TS], rz[0:PT, 0:TS], AF.Ln, scale=1.0 / DS, bias=epsc[0:PT, :])
        act(rz[0:PT, 0:TS], rz[0:PT, 0:TS], AF.Exp, scale=-0.5)
        if sample:
            stt("dve", gzS[0:PT, :], gzS[0:PT, :], rz[0:PT, 0:1], gsgu_bc[0:PT, :], ALU.mult, ALU.mult)
            out_dmas.append(dma(vs_o[:, :], gzS[0:PT, :]))
        ymT = hT

        def sgu_prep(ts):
            wS = wspS[ts % 2]
            ts_("dve", wS[0:PT, :, 0:PT], wspT[0:PT, :, 0:PT], rz[0:PT, ts:ts + 1], None, ALU.mult)

        def sgu(ts):
            wS = wspS[ts % 2]
            for half in range(2):
                pg = ps[6 + half][:, :].rearrange("p (c t) -> p c t", c=4)
                pg4 = ps[6 + half][:, :].rearrange("p (a b t) -> p a b t", a=2, b=2)
                bh = bsp_hi[0:1, 2 * half * 128:(2 * half + 2) * 128].rearrange("p (g t) -> p g t", g=2)[:, :, 0:PT]
                bl = bsp_lo[0:1, 2 * half * 128:(2 * half + 2) * 128].rearrange("p (g t) -> p g t", g=2)[:, :, 0:PT]
                mm(pg4[:, :, :, 0:PT], ones_b1[0:1, :], bh.unsqueeze(2).to_broadcast([1, 2, 2, PT]), start=True, stop=False)
                mm(pg4[:, :, :, 0:PT], ones_b1[0:1, :], bl.unsqueeze(2).to_broadcast([1, 2, 2, PT]), start=False, stop=False)
                for c4 in range(4):
                    cc = half * 4 + c4
                    g = cc // 2
                    mm(pg[:, c4, 0:PT], zs[0:PT, ts, cc * 128:(cc + 1) * 128], wS[0:PT, g, 0:PT], start=False, stop=(c4 == 3))
                tt("dve", ymT[:, 8 + half * 4:8 + half * 4 + 4, ts * 128:ts * 128 + PT], pg[:, :, 0:PT],
                   uTg[:, half * 4:half * 4 + 4, ts * 128:ts * 128 + PT], ALU.mult)

        mark('chunks')
        memset("pool", vv[:, :, :, DV:DV + 1], 1.0)
        def nsb_of(c):
            nb_ = tmpA if c % 2 == 0 else Wbc
            return [nb_[0][:, 0:512].rearrange("p (h d) -> p h d", h=2), nb_[1][:, 0:512].rearrange("p (h d) -> p h d", h=2)]

        def post(c, defer_pe=False):
            j = c
            nsb = nsb_of(c)
            dS = denS[:, c % 2, :]
            memset("pool", ssn[:], 0.0)
            for h in range(NH):
                act(junk2[0:PT, 0:DV], nsb[h // 2][0:PT, h % 2, :], AF.Square, accum=ssn[0:PT, h:h + 1])
            act(sm_a[0:PT, :], dS[0:PT, :], AF.Abs)
            tt("dve", sm_a[0:PT, :], sm_a[0:PT, :], cols[0:PT, j, 4:8], ALU.max)
            A("dve", lambda e: e.reciprocal(out=sm_a[0:PT, :], in_=sm_a[0:PT, :]),
              reads=[sm_a[0:PT, :]], writes=[sm_a[0:PT, :]])
            tt("dve", sm_b[0:PT, :], sm_a[0:PT, :], sm_a[0:PT, :], ALU.mult)
            tt("dve", sm_b[0:PT, :], sm_b[0:PT, :], ssn[0:PT, :], ALU.mult)
            act(sm_b[0:PT, :], sm_b[0:PT, :], AF.Ln, scale=1.0 / DV, bias=epsc[0:PT, :])
            act(sm_b[0:PT, :], sm_b[0:PT, :], AF.Exp, scale=-0.5)
            tt("dve", sm_f[0:PT, :], sm_a[0:PT, :], sm_b[0:PT, :], ALU.mult)
            for h in range(NH):
                stt("dve", ym[0:PT, j, h * DV:(h + 1) * DV], nsb[h // 2][0:PT, h % 2, :], sm_f[0:PT, h:h + 1],
                    gso[0:PT, j, h * DV:(h + 1) * DV], ALU.mult, ALU.mult)
            if not defer_pe:
                post_pe(c)

        def post_pe(c):
            j = c
            pb7 = ps[7][:, 0:512].bitcast(BF16).rearrange("p (k t) -> p k t", k=8)
            for k8 in range(8):
                A("pe", lambda e, k8=k8, j=j: e.transpose(out=pb7[:, k8, 0:PT], in_=ym[0:PT, j, k8 * 128:(k8 + 1) * 128],
                                                           identity=identb[0:PT, 0:PT]),
                  reads=[ym[0:PT, j, k8 * 128:(k8 + 1) * 128], identb[0:PT, 0:PT]], writes=[pb7[:, k8, 0:PT]])
            cp_("act", ymT[:, 0:8, j * 128:j * 128 + PT], pb7[:, :, 0:PT])

        for c in range(NCH):
            j = c
            sgu_prep(j)
            for h in range(NH):
                nump = ps[h][0:PT, 0:DV + 1]
                qc = qh[:, h, c * LC:(c + 1) * LC]
                sc = sT[0:PT, h, j, 0:LC]
                mm(nump, qc, Cbf[:, h, :], start=True, stop=False)
                mm(nump, sc, vv[0:PT, j, h, :], start=False, stop=True)
                up = ps[4 + (h % 2)]
                mm(up[:, 0:DV + 1], kw[0:PT, j, h, :], vv[0:PT, j, h, :])
                stt("dve", Cst[:, h, :], Cst[:, h, :], dec[:, h, c:c + 1], up[:, 0:DV + 1], ALU.mult, ALU.add)
                cp_("act", Cbf[:, h, :], Cst[:, h, :])
            nsb = nsb_of(c)
            for h in range(NH):
                cp_("act" if h % 2 == 0 else "dve", nsb[h // 2][0:PT, h % 2, :], ps[h][0:PT, 0:DV])
                cp_("act" if h % 2 == 0 else "dve", denS[0:PT, c % 2, h:h + 1], ps[h][0:PT, DV:DV + 1])
            if c >= 2:
                post_pe(c - 2)
            if c >= 1:
                post(c - 1, defer_pe=True)
            sgu(j)
        post(NCH - 1, defer_pe=True)
        mark('phase 4: output projection')
        def oproj(cs, slot, ts):
            if cs == 0 and NCH >= 2 and ts == TS - 2:
                post_pe(NCH - 2)
            if cs == 0 and ts == TS - 1:
                post_pe(NCH - 1)
            for k in range(KC):
                mm(ps[ts][0:PT, :], ymT[:, k, ts * 128:ts * 128 + PT], slot[:, k, :], start=(k == 0), stop=(k == KC - 1))
            xs_ = xt[0:PT, ts, cs * 512:(cs + 1) * 512]
            tt("dve", xs_, xs_, ps[ts][0:PT, :], ALU.add)

        for cs in range(2):
            slot = load_unit("wo%d" % cs)
            for ts in range(TS):
                oproj(cs, slot, ts)
        if wstate["fused"] or TS == 1:
            for cs in range(2, 4):
                slot = load_unit("wo%d" % cs)
                for ts in range(TS):
                    oproj(cs, slot, ts)
        else:
            slots23 = [(2, load_unit("wo2")), (3, load_unit("wo3"))]
            for cs, slot in slots23:
                for ts in range(TS - 1):
                    oproj(cs, slot, ts)
            for cs, slot in slots23:
                oproj(cs, slot, TS - 1)

        if last:
            if sample:
                out_dmas.append(dma(cs_o.rearrange("h d e -> d h e"), Cst[:, :, 0:DV]))
                out_dmas.append(dma(nst_o[:, :], Cst[:, :, DV], slow=True))
                out_dmas.append(dma(ms_o[:, :], mlast[:]))
            else:
                out_dmas.append(dma(cp[seq].rearrange("h d e -> d h e"), Cst[:, :, 0:DV]))
                out_dmas.append(dma(npt[seq], Cst[:, :, DV], slow=True))
                out_dmas.append(dma(mp[seq], mlast[:]))

        if wstate["fused"]:
            pump(UID["wo3"] + 3)
        mark('phase 5: norm2')
        n2_last = rmsnorm_to_hT(ss1, rs1, defer_last=(TS == 4))

        mark('phase 6: ff1')
        for u in range(16):
            slot = load_unit("f1_%d" % u)
            if u == 0 and TS == 4:
                T3 = 3 * 128
                for sub in range(4):
                    for k in range(KC):
                        mm(ps[sub][:, 0:T3], slot[:, k, sub * 128:(sub + 1) * 128], hT[:, k, 0:T3],
                           start=(k == 0), stop=(k == KC - 1))
                n2_last()
                for sub in range(4):
                    for k in range(KC):
                        mm(ps[sub][:, T3:T], slot[:, k, sub * 128:(sub + 1) * 128], hT[:, k, T3:T],
                           start=(k == 0), stop=(k == KC - 1))
            for sub in range(4):
                p_ = ps[sub]
                if not (u == 0 and TS == 4):
                    for k in range(KC):
                        mm(p_[:, 0:T], slot[:, k, sub * 128:(sub + 1) * 128], hT[:, k, 0:T], start=(k == 0), stop=(k == KC - 1))
                t_ = tmpA[cnt["a"] % 2]
                cnt["a"] += 1
                act(t_[:, 0:T], p_[:, 0:T], AF.Relu)
                tt("dve" if sub % 2 == 0 else "pool", hid[:, u * 4 + sub, 0:T], t_[:, 0:T], t_[:, 0:T], ALU.mult)

        mark('phase 7: ff2')
        hnA = tmpA[0][:, :].bitcast(BF16)
        hnB = tmpA[1][:, :].bitcast(BF16)

        xstg_ = xstg_ov if xstg_ov is not None else xstg

        def pro_a(ts):
            xstg = xstg_
            dma(xstg[:, :], nxt(ts))
            memset("pool", ssP2[:, :], 0.0)
            act(hnA[:, :], xstg[:, 0:1024], AF.Square, accum=ssP2[:, 0:1])
            act(hnB[:, :], xstg[:, 1024:2048], AF.Square, accum=ssP2[:, 1:2])
            tt("dve", ssP[:, ts:ts + 1], ssP2[:, 0:1], ssP2[:, 1:2], ALU.add)
            act(rsP[:, ts:ts + 1], ssP[:, ts:ts + 1], AF.Ln, scale=1.0 / D, bias=epsc[:, :])
            act(rsP[:, ts:ts + 1], rsP[:, ts:ts + 1], AF.Exp, scale=-0.5)
            ts_("dve", hnA[:, :], xstg[:, 0:1024], rsP[:, ts:ts + 1], None, ALU.mult)
            ts_("dve", hnB[:, :], xstg[:, 1024:2048], rsP[:, ts:ts + 1], None, ALU.mult)

        def pro_b(ts):
            for half in range(2):
                src_ = hnA if half == 0 else hnB
                pb = ps[4 + half][:, 0:512].bitcast(BF16).rearrange("p (k t) -> p k t", k=8)
                for k8 in range(8):
                    A("pe", lambda e, pb=pb, k8=k8, src_=src_: e.transpose(
                        out=pb[:, k8, :], in_=src_[:, k8 * 128:(k8 + 1) * 128], identity=identb[:, :]),
                      reads=[src_[:, k8 * 128:(k8 + 1) * 128], identb[:, :]], writes=[pb[:, k8, :]])
                cp_("act" if half == 0 else "dve", hT[:, half * 8:(half + 1) * 8, ts * 128:(ts + 1) * 128], pb[:, :, :])

        dma(gf_bc[:, :], gf_d[0:1, :].to_broadcast([128, D]))
        for cs in range(4):
            for q in range(4):
                if nxt is not None and cs >= 2:
                    step = (cs - 2) * 4 + q
                    if step % 2 == 0:
                        pro_a(step // 2)
                    else:
                        pro_b(step // 2)
                slot = load_unit("f2_%d_%d" % (cs, q))
                for ts in range(TS):
                    for k in range(KC):
                        mm(ps[ts][0:PT, :], hid[:, q * 16 + k, ts * 128:ts * 128 + PT], slot[:, k, :],
                           start=(q == 0 and k == 0), stop=(q == 3 and k == KC - 1))
                    if q == 3:
                        xs_ = xt[0:PT, ts, cs * 512:(cs + 1) * 512]
                        tt("dve", xs_, xs_, ps[ts][0:PT, :], ALU.add)

        if nxt is not None:
            prefetch_unit("v0")
            prefetch_unit("v1")
        mark('phase 8: final norm')
        memset("pool", ss1[:, :], 0.0)
        for ts in range(TS):
            act(junk[0:PT, :], xt[0:PT, ts, :], AF.Square, accum=ss1[0:PT, ts:ts + 1])
            act(rs1[0:PT, ts:ts + 1], ss1[0:PT, ts:ts + 1], AF.Ln, scale=1.0 / D, bias=epsc[0:PT, :])
            act(rs1[0:PT, ts:ts + 1], rs1[0:PT, ts:ts + 1], AF.Exp, scale=-0.5)
            stt("dve", xt[0:PT, ts, :], xt[0:PT, ts, :], rs1[0:PT, ts:ts + 1], gf_bc[0:PT, :], ALU.mult, ALU.mult)
            out_dmas.append(dma(ydst(ts), xt[0:PT, ts, :]))

    def xsrc_of(seq, tl):
        t0 = tl * 512
        return lambda ts, seq=seq, t0=t0: xp[seq, t0 + ts * 128:t0 + (ts + 1) * 128, :]

    if fused:
        wstate["fused"] = True
        run_tile(64, lambda ts: xs[:, :], lambda ts: ys[:, :], first=True, last=True, kind="sample", seq=0,
                 nxt=(xsrc_of(0, 0) if pipeline else None), xstg_ov=bview(36864, 4096, F32))
        wstate["fused"] = False
        pump(len(units) - 1)
        S.freeze("wscr")
    mark('prompt tiles')
    tiles = [(seq, tl) for seq in range(n_seq) for tl in range(n_prompt_tiles)]

    for i, (seq, tl) in enumerate(tiles):
        t0 = tl * 512
        nxt = xsrc_of(*tiles[i + 1]) if (pipeline and i + 1 < len(tiles)) else None
        run_tile(512, xsrc_of(seq, tl),
                 lambda ts, seq=seq, t0=t0: yp[seq, t0 + ts * 128:t0 + (ts + 1) * 128, :],
                 first=(tl == 0), last=(tl == n_prompt_tiles - 1), kind="prompt", seq=seq,
                 skip_norm1=(pipeline and (i > 0 or fused)), nxt=nxt)
    if max_ins is not None:
        S.ins = S.ins[:max_ins]
        out_dmas = [k for k, it in enumerate(S.ins) if it["dma"]]
    A("sp", None, extra_deps=out_dmas)
    info = S.emit(st)
    info['marks'] = marks
    st.close()
    return nc, info


_CACHE = {}
NCORES = 8


def kernel(x_prompt, x_sample, state_mlstm_C, state_mlstm_n, state_mlstm_m, w_in, b_gate, g_mh, g_sgu,
           w_sp, b_sp, w_out, g_norm1, g_norm2, w_ff1, w_ff2, g_final):
    f = lambda a: np.ascontiguousarray(np.asarray(a, dtype=np.float32))
    x_prompt = f(x_prompt); x_sample = f(x_sample)
    C0 = f(state_mlstm_C)[0]; n0 = f(state_mlstm_n)[0]; m0 = f(state_mlstm_m)[0]
    w_in0 = f(w_in)[0]; w_out0 = f(w_out)[0]; w_ff10 = f(w_ff1)[0]; w_ff20 = f(w_ff2)[0]
    bg = f(b_gate)[0]
    if "nc" not in _CACHE:
        _CACHE["nc"] = build_program()[0]
    nc = _CACHE["nc"]
    shared = {
        "w_in": w_in0, "w_out": w_out0, "w_ff1": w_ff10, "w_ff2": w_ff20,
        "bgi": np.ascontiguousarray(bg[0:4].reshape(4, 1)), "bgf": np.ascontiguousarray(bg[4:8].reshape(4, 1)),
        "g_mh": f(g_mh)[0].reshape(1, DM), "g_sgu": f(g_sgu)[0].reshape(1, DS),
        "w_sp": f(w_sp)[0], "b_sp": f(b_sp)[0].reshape(1, 512),
        "g1t": np.ascontiguousarray(f(g_norm1)[0].reshape(KC, 128).T),
        "g2t": np.ascontiguousarray(f(g_norm2)[0].reshape(KC, 128).T),
        "g_final": f(g_final).reshape(1, D),
    }
    in_maps = []
    for i in range(NCORES):
        m = dict(shared)
        m["xp"] = x_prompt[2 * i:2 * i + 2]
        m["xs"] = x_sample[i]
        m["c0"] = C0[i]
        m["n0t"] = np.ascontiguousarray(n0[i].T)
        m["m0"] = np.ascontiguousarray(m0[i].reshape(NH, 1))
        in_maps.append(m)
    res = run_bass_kernel_spmd(nc, in_maps, core_ids=list(range(NCORES)))
    R = list(res.results) + [res.results[0]] * (8 - NCORES)
    y_prompt = np.concatenate([R[i]["yp"] for i in range(8)], axis=0)
    y_sample = np.stack([R[i]["ys"] for i in range(8)], axis=0)
    C_prompt = np.concatenate([R[i]["cp"] for i in range(8)], axis=0)[None]
    n_prompt = np.concatenate([np.transpose(R[i]["npt"], (0, 2, 1)) for i in range(8)], axis=0)[None]
    m_prompt = np.concatenate([R[i]["mp"].reshape(2, NH) for i in range(8)], axis=0)[None]
    C_sample = np.stack([R[i]["cs"] for i in range(8)], axis=0)[None]
    n_sample = np.stack([R[i]["nst"].T for i in range(8)], axis=0)[None]
    m_sample = np.stack([R[i]["ms"].reshape(NH) for i in range(8)], axis=0)[None]
    sgu_v = np.stack([R[i]["vs"] for i in range(8)], axis=0)[None]
    outs = (y_prompt, y_sample, C_prompt, n_prompt, m_prompt, C_sample, n_sample, m_sample, sgu_v)
    return tuple(np.ascontiguousarray(o, dtype=np.float32) for o in outs)
```
